# Optimizing a Trainium2 kernel written in Bass

```python
import math
import jax, jax.numpy as jnp
from jax import lax
import numpy as np

D_MODEL = 2048
BATCH = 8
SEQ = 2048
DEPTH = 1
DEC_BATCH = 16
DEC_SEQ = 32
PAST_LEN = 1024

CHUNK = 64
Q_BLOCK = 128
N_MEM = 256
EPS = 1e-6
NEG_INF = -1e30

GLA_HEADS = 4
GLA_DK = 128
GLA_DV = 256
GLA_GATE_RANK = 16
GLA_GATE_NORM = 16.0
DIFF_HEADS = 4
DIFF_DH = 64
DIFF_DV = 128
MEM_HEADS = 4
MEM_DH = 128
REL_BUCKETS = 32
REL_MAX_DIST = 128
D_FF = 5504

GLA_QK = GLA_HEADS * GLA_DK
GLA_V = GLA_HEADS * GLA_DV
DIFF_QK = DIFF_HEADS * 2 * DIFF_DH
DIFF_V = DIFF_HEADS * DIFF_DV
MEM_W = MEM_HEADS * MEM_DH
MIX_WIDTH = GLA_V + DIFF_V + MEM_W
SPLITS = (GLA_QK, GLA_QK, GLA_V, GLA_V, GLA_GATE_RANK, DIFF_QK, DIFF_QK, DIFF_V, MEM_W)
IN_WIDTH = GLA_QK * 2 + GLA_V * 2 + GLA_GATE_RANK + DIFF_QK * 2 + DIFF_V + MEM_W

kernel_name = 'hybrid_gla_diffattn_memory_macaron_step'


def rmsnorm(x, g):
    xf = x.astype(jnp.float32)
    y = xf * lax.rsqrt(jnp.mean(xf * xf, axis=-1, keepdims=True) + EPS)
    return (y * g.astype(jnp.float32)).astype(x.dtype)


def swiglu(x, w_in, w_out):
    gate, up = jnp.split(x @ w_in, 2, axis=-1)
    return (jax.nn.silu(gate) * up) @ w_out


def rel_bucket(rel):
    nb = REL_BUCKETS // 2
    max_exact = nb // 2
    ret = jnp.where(rel > 0, nb, 0)
    n = jnp.abs(rel)
    nf = jnp.maximum(n, 1).astype(jnp.float32)
    large = max_exact + (jnp.log(nf / max_exact) / math.log(REL_MAX_DIST / max_exact)
                         * (nb - max_exact)).astype(jnp.int32)
    large = jnp.minimum(large, nb - 1)
    return ret + jnp.where(n < max_exact, n, large)


def rel_bias(q_pos, k_pos, table):
    b = rel_bucket(k_pos[None, :] - q_pos[:, None])
    return jnp.transpose(table.astype(jnp.float32)[b], (2, 0, 1))


def project_groups(h, lp):
    B, T, _ = h.shape
    idx = np.cumsum(np.array(SPLITS))[:-1].tolist()
    gq, gk, gv, gr, glr, dq, dk, dv, mq = jnp.split(h @ lp['w_in'], idx, axis=-1)
    f32 = jnp.float32
    gq = gq.reshape(B, T, GLA_HEADS, GLA_DK).astype(f32) * (GLA_DK ** -0.5)
    gk = gk.reshape(B, T, GLA_HEADS, GLA_DK).astype(f32)
    gv = gv.reshape(B, T, GLA_HEADS, GLA_DV).astype(f32)
    gg = jax.nn.log_sigmoid((glr @ lp['w_gla_g2'] + lp['b_gla_g']).astype(f32)) / GLA_GATE_NORM
    gg = gg.reshape(B, T, GLA_HEADS, GLA_DK)
    dq = rmsnorm(dq.reshape(B, T, DIFF_HEADS, 2, DIFF_DH), lp['diff_q_norm'])
    dk = rmsnorm(dk.reshape(B, T, DIFF_HEADS, 2, DIFF_DH), lp['diff_k_norm'])
    dv = dv.reshape(B, T, DIFF_HEADS, DIFF_DV)
    mq = rmsnorm(mq.reshape(B, T, MEM_HEADS, MEM_DH), lp['mem_q_norm'])
    return gq, gk, gv, gr, gg, dq, dk, dv, mq


def gla_chunk(s0, q, k, v, g):
    b = jnp.cumsum(g, axis=2)
    qe = q * jnp.exp(b)
    ke = k * jnp.exp(-b)
    C = q.shape[2]
    causal = jnp.tril(jnp.ones((C, C), dtype=bool))
    a = jnp.where(causal, jnp.einsum('bhid,bhjd->bhij', qe, ke), 0.0)
    o = jnp.einsum('bhcd,bhde->bhce', qe, s0) + jnp.einsum('bhij,bhje->bhie', a, v)
    b_last = b[:, :, -1:, :]
    s1 = jnp.exp(b_last[:, :, 0, :, None]) * s0 + jnp.einsum('bhjd,bhje->bhde', k * jnp.exp(b_last - b), v)
    return s1, o


def diff_lambda_value(lam_params, layer):
    l = lam_params.astype(jnp.float32)
    lam_init = 0.8 - 0.6 * math.exp(-0.3 * layer)
    lam = jnp.exp(jnp.sum(l[0] * l[1])) - jnp.exp(jnp.sum(l[2] * l[3])) + lam_init
    return lam, lam_init


def diff_attend(q, k, v, q_pos, k_pos, lam, table):
    s = jnp.einsum('bqhcd,bkhcd->bhcqk', q.astype(jnp.float32), k.astype(jnp.float32)) * (DIFF_DH ** -0.5)
    s = s + rel_bias(q_pos, k_pos, table)[None, :, None]
    visible = (k_pos[None, :] // CHUNK) <= (q_pos[:, None] // CHUNK)
    p = jax.nn.softmax(jnp.where(visible, s, NEG_INF), axis=-1)
    a = p[:, :, 0] - lam * p[:, :, 1]
    return jnp.einsum('bhqk,bkhd->bqhd', a, v.astype(jnp.float32))


def mem_kv(mem, lp):
    B, M, _ = mem.shape
    k, v = jnp.split(rmsnorm(mem, lp['mem_norm']) @ lp['w_mem_kv'], 2, axis=-1)
    k = rmsnorm(k.reshape(B, M, MEM_HEADS, MEM_DH), lp['mem_k_norm'])
    return k, v.reshape(B, M, MEM_HEADS, MEM_DH)


def mem_attend(q, k, v):
    s = jnp.einsum('bqhd,bmhd->bhqm', q.astype(jnp.float32), k.astype(jnp.float32)) * (MEM_DH ** -0.5)
    p = jax.nn.softmax(s, axis=-1)
    return jnp.einsum('bhqm,bmhd->bqhd', p, v.astype(jnp.float32))


def merge(gla_o, gla_r, diff_o, mem_o, lam_init, lp, dtype):
    B, T = gla_o.shape[:2]
    g = rmsnorm(gla_o, lp['gla_out_norm']) * jax.nn.silu(gla_r.reshape(B, T, GLA_HEADS, GLA_DV).astype(jnp.float32))
    d = rmsnorm(diff_o, lp['diff_out_norm']) * (1.0 - lam_init)
    cat = jnp.concatenate([g.reshape(B, T, GLA_V), d.reshape(B, T, DIFF_V), mem_o.reshape(B, T, MEM_W)], axis=-1)
    return cat.astype(dtype) @ lp['w_o']


def mix_prompt(h, mem, lp, table, layer):
    B, S, _ = h.shape
    gq, gk, gv, gr, gg, dq, dk, dv, mq = project_groups(h, lp)
    n = S // CHUNK

    def to_chunks(t):
        return t.reshape(B, n, CHUNK, t.shape[2], t.shape[3]).transpose(1, 0, 3, 2, 4)

    s0 = jnp.zeros((B, GLA_HEADS, GLA_DK, GLA_DV), jnp.float32)
    gla_state, oc = lax.scan(lambda st, xs: gla_chunk(st, xs[0], xs[1], xs[2], xs[3]), s0,
                             (to_chunks(gq), to_chunks(gk), to_chunks(gv), to_chunks(gg)))
    gla_o = oc.transpose(1, 0, 3, 2, 4).reshape(B, S, GLA_HEADS, GLA_DV)
    lam, lam_init = diff_lambda_value(lp['diff_lambda'], layer)
    nb = S // Q_BLOCK
    k_pos = jnp.arange(S)
    q_blocks = dq.reshape(B, nb, Q_BLOCK, DIFF_HEADS, 2, DIFF_DH).swapaxes(0, 1)
    pos_blocks = k_pos.reshape(nb, Q_BLOCK)
    ob = lax.map(lambda xs: diff_attend(xs[0], dk, dv, xs[1], k_pos, lam, table), (q_blocks, pos_blocks))
    diff_o = ob.swapaxes(0, 1).reshape(B, S, DIFF_HEADS, DIFF_DV)
    mk, mv = mem_kv(mem, lp)
    mem_o = mem_attend(mq, mk, mv)
    y = merge(gla_o, gr, diff_o, mem_o, lam_init, lp, h.dtype)
    return y, (dk.reshape(B, S, DIFF_HEADS, 2 * DIFF_DH), dv, gla_state, mk, mv)


def mix_sample(h, past_k, past_v, gla_s0, mem_k, mem_v, lp, table, layer):
    B, T, _ = h.shape
    P = past_k.shape[1]
    gq, gk, gv, gr, gg, dq, dk, dv, mq = project_groups(h, lp)
    gla_state, o = gla_chunk(gla_s0.astype(jnp.float32), gq.transpose(0, 2, 1, 3), gk.transpose(0, 2, 1, 3),
                             gv.transpose(0, 2, 1, 3), gg.transpose(0, 2, 1, 3))
    gla_o = o.transpose(0, 2, 1, 3)
    lam, lam_init = diff_lambda_value(lp['diff_lambda'], layer)
    k_all = jnp.concatenate([past_k.reshape(B, P, DIFF_HEADS, 2, DIFF_DH).astype(dk.dtype), dk], axis=1)
    v_all = jnp.concatenate([past_v.astype(dv.dtype), dv], axis=1)
    q_pos = P + jnp.arange(T)
    k_pos = jnp.arange(P + T)
    diff_o = diff_attend(dq, k_all, v_all, q_pos, k_pos, lam, table)
    mem_o = mem_attend(mq, mem_k, mem_v)
    y = merge(gla_o, gr, diff_o, mem_o, lam_init, lp, h.dtype)
    return y, (dk.reshape(B, T, DIFF_HEADS, 2 * DIFF_DH), dv, gla_state)


def conformer_layer(x, mix, lp):
    x = x + 0.5 * swiglu(rmsnorm(x, lp['norm_ffn1']), lp['w_ffn1_in'], lp['w_ffn1_out'])
    y, state = mix(rmsnorm(x, lp['norm_mix']))
    x = x + y
    x = x + 0.5 * swiglu(rmsnorm(x, lp['norm_ffn2']), lp['w_ffn2_in'], lp['w_ffn2_out'])
    return rmsnorm(x, lp['norm_final']), state


def setup_inputs(seed: int = 0) -> dict:
    key = jax.random.key(seed)
    ks = list(jax.random.split(key, 32))
    f32 = jnp.float32

    def nrm(shape, scale):
        return jax.random.normal(ks.pop(), shape, f32) * scale

    def gain(shape):
        return 1.0 + nrm(shape, 0.02)

    D, L = D_MODEL, DEPTH
    return {
        'x_prompt': nrm((BATCH, SEQ, D), 1.0),
        'x_sample': nrm((DEC_BATCH, DEC_SEQ, D), 1.0),
        'mem_prompt': nrm((BATCH, N_MEM, D), 1.0),
        'cache_diff_k': nrm((L, DEC_BATCH, PAST_LEN, DIFF_HEADS, 2 * DIFF_DH), 1.0),
        'cache_diff_v': nrm((L, DEC_BATCH, PAST_LEN, DIFF_HEADS, DIFF_DV), 1.0),
        'state_gla': nrm((L, DEC_BATCH, GLA_HEADS, GLA_DK, GLA_DV), 0.5),
        'cache_mem_k': nrm((L, DEC_BATCH, N_MEM, MEM_HEADS, MEM_DH), 1.0),
        'cache_mem_v': nrm((L, DEC_BATCH, N_MEM, MEM_HEADS, MEM_DH), 1.0),
        'rel_bias_table': nrm((REL_BUCKETS, DIFF_HEADS), 0.5),
        'norm_ffn1': gain((L, D)),
        'w_ffn1_in': nrm((L, D, 2 * D_FF), D ** -0.5),
        'w_ffn1_out': nrm((L, D_FF, D), D_FF ** -0.5),
        'norm_mix': gain((L, D)),
        'w_in': nrm((L, D, IN_WIDTH), D ** -0.5),
        'w_gla_g2': nrm((L, GLA_GATE_RANK, GLA_QK), GLA_GATE_RANK ** -0.5),
        'b_gla_g': nrm((L, GLA_QK), 0.1),
        'gla_out_norm': gain((L, GLA_DV)),
        'diff_q_norm': gain((L, DIFF_DH)),
        'diff_k_norm': gain((L, DIFF_DH)),
        'diff_lambda': nrm((L, 4, DIFF_DH), 0.1),
        'diff_out_norm': gain((L, DIFF_DV)),
        'mem_norm': gain((L, D)),
        'w_mem_kv': nrm((L, D, 2 * MEM_W), D ** -0.5),
        'mem_q_norm': gain((L, MEM_DH)),
        'mem_k_norm': gain((L, MEM_DH)),
        'w_o': nrm((L, MIX_WIDTH, D), MIX_WIDTH ** -0.5),
        'norm_ffn2': gain((L, D)),
        'w_ffn2_in': nrm((L, D, 2 * D_FF), D ** -0.5),
        'w_ffn2_out': nrm((L, D_FF, D), D_FF ** -0.5),
        'norm_final': gain((L, D)),
    }


def reference(x_prompt, x_sample, mem_prompt, cache_diff_k, cache_diff_v, state_gla, cache_mem_k, cache_mem_v,
              rel_bias_table, norm_ffn1, w_ffn1_in, w_ffn1_out, norm_mix, w_in, w_gla_g2, b_gla_g, gla_out_norm,
              diff_q_norm, diff_k_norm, diff_lambda, diff_out_norm, mem_norm, w_mem_kv, mem_q_norm, mem_k_norm,
              w_o, norm_ffn2, w_ffn2_in, w_ffn2_out, norm_final):
    yp, ys = x_prompt, x_sample
    pk, pv, pg, pmk, pmv, sk, sv, sg = [], [], [], [], [], [], [], []
    for l in range(DEPTH):
        lp = {
            'norm_ffn1': norm_ffn1[l], 'w_ffn1_in': w_ffn1_in[l], 'w_ffn1_out': w_ffn1_out[l],
            'norm_mix': norm_mix[l], 'w_in': w_in[l], 'w_gla_g2': w_gla_g2[l], 'b_gla_g': b_gla_g[l],
            'gla_out_norm': gla_out_norm[l], 'diff_q_norm': diff_q_norm[l], 'diff_k_norm': diff_k_norm[l],
            'diff_lambda': diff_lambda[l], 'diff_out_norm': diff_out_norm[l], 'mem_norm': mem_norm[l],
            'w_mem_kv': w_mem_kv[l], 'mem_q_norm': mem_q_norm[l], 'mem_k_norm': mem_k_norm[l], 'w_o': w_o[l],
            'norm_ffn2': norm_ffn2[l], 'w_ffn2_in': w_ffn2_in[l], 'w_ffn2_out': w_ffn2_out[l],
            'norm_final': norm_final[l],
        }
        yp, (dk_p, dv_p, g_p, mk_p, mv_p) = conformer_layer(
            yp, lambda h: mix_prompt(h, mem_prompt, lp, rel_bias_table, l), lp)
        ys, (dk_s, dv_s, g_s) = conformer_layer(
            ys, lambda h: mix_sample(h, cache_diff_k[l], cache_diff_v[l], state_gla[l], cache_mem_k[l],
                                     cache_mem_v[l], lp, rel_bias_table, l), lp)
        pk.append(dk_p); pv.append(dv_p); pg.append(g_p); pmk.append(mk_p); pmv.append(mv_p)
        sk.append(dk_s); sv.append(dv_s); sg.append(g_s)
    return (yp, ys, jnp.stack(pk), jnp.stack(pv), jnp.stack(pg), jnp.stack(pmk), jnp.stack(pmv),
            jnp.stack(sk), jnp.stack(sv), jnp.stack(sg))
```

```python
import contextlib
import types
import numpy as np
import concourse.bass as bass
import concourse.mybir as mybir
from concourse.bass_utils import run_bass_kernel_spmd

F32 = mybir.dt.float32
BF16 = mybir.dt.bfloat16
AF = mybir.ActivationFunctionType
ALU = mybir.AluOpType
AX = mybir.AxisListType

EPOCH = 6000
DO_COMPILE = False
DMA_EPOCH = 300

D = 2048
KC = 16
FF = 5504
NFF = 43
SEQ = 2048
PAST = 1024
NMEM = 256
INW = 5136
EPS = 1e-6
LAM_INIT = 0.2
C_GQ, C_GK, C_GV, C_GR, C_GLR, C_DQ, C_DK, C_DV, C_MQ = 0, 512, 1024, 2048, 3072, 3088, 3600, 4112, 4624


class _StopBuild(Exception):
    pass


class Op:
    __slots__ = ("eng", "fn", "reads", "writes", "dma", "idx", "waits", "signal", "sem", "val")

    def __init__(self, eng, fn, reads, writes, dma):
        self.eng = eng
        self.fn = fn
        self.reads = tuple(reads)
        self.writes = tuple(writes)
        self.dma = dma
        self.waits = []
        self.signal = False
        self.sem = None
        self.val = 0


def _freeze(fn, depth=0):
    if not isinstance(fn, types.FunctionType) or fn.__closure__ is None or depth > 3:
        return fn
    cells = []
    for c in fn.__closure__:
        try:
            v = c.cell_contents
        except ValueError:
            cells.append(c)
            continue
        if isinstance(v, types.FunctionType) and v.__code__.co_filename == fn.__code__.co_filename:
            v = _freeze(v, depth + 1)
        cells.append(types.CellType(v))
    g = types.FunctionType(fn.__code__, fn.__globals__, fn.__name__, fn.__defaults__, tuple(cells))
    g.__kwdefaults__ = fn.__kwdefaults__
    return g


class Prog:
    ENGS = ("pe", "act", "dve", "pool", "sp")

    def __init__(self):
        self.ops = []

    def add(self, eng, fn, reads=(), writes=(), dma=None):
        fn = _freeze(fn)
        op = Op(eng, fn, reads, writes, dma)
        op.idx = len(self.ops)
        self.ops.append(op)
        return op

    def pe(self, fn, reads=(), writes=()):
        return self.add("pe", fn, reads, writes)

    def act(self, fn, reads=(), writes=()):
        return self.add("act", fn, reads, writes)

    def dve(self, fn, reads=(), writes=()):
        return self.add("dve", fn, reads, writes)

    def pool(self, fn, reads=(), writes=()):
        return self.add("pool", fn, reads, writes)

    def dma(self, queue, sem, fn, reads=(), writes=()):
        return self.add(queue, fn, reads, writes, dma=sem)

    def analyze(self):
        last_w = {}
        rd_eng = {}
        rd_dma = {}
        dep_lists = []
        for op in self.ops:
            deps = {}

            def add(d, kind):
                if d is op:
                    return
                if d.dma is None and op.dma is None and d.eng == op.eng:
                    if op.eng == "pe":
                        return
                deps[d.idx] = d

            for k in op.reads:
                w = last_w.get(k)
                if w is not None:
                    add(w, "RAW")
            for k in op.writes:
                w = last_w.get(k)
                if w is not None:
                    add(w, "WAW")
                for r in rd_eng.get(k, {}).values():
                    add(r, "WAR")
                for r in rd_dma.get(k, ()):
                    add(r, "WAR")
            for k in op.reads:
                if op.dma is not None:
                    rd_dma.setdefault(k, []).append(op)
                else:
                    rd_eng.setdefault(k, {})[op.eng] = op
            for k in op.writes:
                last_w[k] = op
                rd_eng[k] = {}
                rd_dma[k] = []
            dep_lists.append(list(deps.values()))
        for op, deps in zip(self.ops, dep_lists):
            for d in deps:
                d.signal = True
        cnt = {}
        self.sem_names = []
        for op in self.ops:
            if op.dma is not None:
                key = ("dma", op.dma)
                per = DMA_EPOCH
                step = 16
            elif op.signal:
                key = ("eng", op.eng)
                per = EPOCH
                step = 1
            else:
                continue
            n = cnt.get(key, 0)
            cnt[key] = n + 1
            name = "%s_%s_%d" % (key[0], key[1], n // per)
            if n % per == 0:
                self.sem_names.append(name)
            op.sem = name
            op.val = (n % per + 1) * step
        for op, deps in zip(self.ops, dep_lists):
            w = {}
            for d in deps:
                if w.get(d.sem, 0) < d.val:
                    w[d.sem] = d.val
            if op.dma is not None and op.val > 16:
                if w.get(op.sem, 0) < op.val - 16:
                    w[op.sem] = op.val - 16
            op.waits = sorted(w.items())
        return self

    def emit(self, nc, stack):
        self.analyze()
        sems = {}
        for name in self.sem_names:
            sems[name] = stack.enter_context(nc.semaphore(name))
        by_eng = {e: [op for op in self.ops if op.eng == e] for e in self.ENGS}
        block = stack.enter_context(nc.Block())
        stats = {e: [0, 0] for e in self.ENGS}
        final_vals = {}
        for op in self.ops:
            if op.sem is not None:
                final_vals[op.sem] = max(final_vals.get(op.sem, 0), op.val)

        def run(e, eng):
            seen = {}
            for op in by_eng[e]:
                for (s, v) in op.waits:
                    if seen.get(s, 0) < v:
                        eng.wait_ge(sems[s], v)
                        seen[s] = v
                        stats[e][1] += 1
                inst = op.fn(eng)
                stats[e][0] += 1
                if op.dma is not None:
                    inst.then_inc(sems[op.sem], 16)
                elif op.signal:
                    inst.then_inc(sems[op.sem], 1)

        @block.tensor
        def _(eng):
            run("pe", eng)

        @block.scalar
        def _(eng):
            run("act", eng)

        @block.vector
        def _(eng):
            run("dve", eng)

        @block.gpsimd
        def _(eng):
            run("pool", eng)

        @block.sync
        def _(eng):
            run("sp", eng)
            for name, v in sorted(final_vals.items()):
                if name.startswith("dma_"):
                    eng.wait_ge(sems[name], v)

        self.stats = stats
        return stats


def _bucket(rel):
    nb, max_exact = 16, 8
    ret = np.where(rel > 0, nb, 0)
    n = np.abs(rel)
    nf = np.maximum(n, 1).astype(np.float32)
    large = max_exact + (np.log(nf / max_exact) / np.float32(np.log(128 / max_exact)) * (nb - max_exact)).astype(np.int32)
    large = np.minimum(large, nb - 1)
    return ret + np.where(n < max_exact, n, large)


def _consts():
    k = np.arange(128)[:, None]
    q = np.arange(128)[None, :]
    c = np.zeros((128, 6, 128), np.float32)
    c[:, 0] = np.eye(128)
    c[:, 1] = (k <= q)
    c[:, 2] = (k <= q) * (-1.0 / 16.0)
    c[:, 3] = (k > q) * (-1.0 / 16.0)
    c[:, 4] = _bucket(k - q)
    c[:, 5] = _bucket(k - q - 128)
    return c.reshape(128, 6 * 128)


def build_program(n_prompt_tiles=4, do_sample=True, do_mem=True, stop=None, tiny=False, pad=0, padeng="dve"):
    nc = bass.Bass("TRN2", target_bir_lowering=False)
    P = Prog()

    def din(name, shape):
        if tiny and name in ("w1i", "w1o", "w_in", "wmkv", "wo", "w2i", "w2o", "xp"):
            shape = [128, 128]
        return nc.dram_tensor(name, list(shape), F32, kind="ExternalInput").ap()

    def dout(name, shape):
        return nc.dram_tensor(name, list(shape), F32, kind="ExternalOutput").ap()

    xp = din("xp", [SEQ, D]); xs = din("xs", [64, D]); mem = din("mem", [NMEM, D])
    cdk = din("cdk", [2, PAST, 512]); cdv = din("cdv", [2, PAST, 512]); sgla = din("sgla", [2, 4, 128, 256])
    cmk = din("cmk", [2, NMEM, 512]); cmv = din("cmv", [2, NMEM, 512])
    table = din("table", [128]); nvec = din("nvec", [5, D]); glan = din("glan", [256])
    w1i = din("w1i", [D, 2 * FF]); w1o = din("w1o", [FF, D]); w_in = din("w_in", [D, INW])
    wg2 = din("wg2", [16, 512]); bg = din("bg", [1, 512])
    dqn = din("dqn", [64]); dkn = din("dkn", [64]); dlam = din("dlam", [256]); don = din("don", [128])
    wmkv = din("wmkv", [D, 1024]); mqn = din("mqn", [128]); mkn = din("mkn", [128])
    wo = din("wo", [D, D]); w2i = din("w2i", [D, 2 * FF]); w2o = din("w2o", [FF, D])
    cst = din("cst", [128, 768])
    yp = dout("yp", [SEQ, D]); ys = dout("ys", [64, D])
    dkp = dout("dkp", [SEQ, 512]); dvp = dout("dvp", [SEQ, 512]); gsp = dout("gsp", [4, 128, 256])
    mkp = dout("mkp", [NMEM, 512]); mvp = dout("mvp", [NMEM, 512])
    dks = dout("dks", [64, 512]); dvs = dout("dvs", [64, 512]); gss = dout("gss", [2, 4, 128, 256])

    NW_LOADS = 2 * (22 + 12) + 11 + 8
    wscr = nc.dram_tensor("wscr", [128, NW_LOADS * 16 * 512], BF16).ap()
    st = contextlib.ExitStack()
    with st:
        def sb(name, shape, dt=F32):
            return st.enter_context(nc.sbuf_tensor(name, shape, dt))

        xT = sb("xT", [128, KC, 512])
        hT = sb("hT", [128, KC, 512], BF16)
        NSLOT = 3
        slots = [sb("slot%d" % i, [128, 16, 512], BF16) for i in range(NSLOT)]
        AR_COLS = 22016
        arena = sb("arena", [128, AR_COLS], BF16)
        dkT = sb("dkT", [128, 4, SEQ], BF16)
        vaug = sb("vaug", [128, 16, 4, 132], BF16)
        dkTn = sb("dkTn", [128, 4, 64], BF16)
        vnew = sb("vnew", [32, 2, 4, 132], BF16)
        mkT = [sb("mkT0", [128, 4, NMEM], BF16), None]
        mvaug = [sb("mvaug0", [128, 2, 4, 132], BF16), None]
        Sst = [sb("Sst0", [128, 4, 256]), None]
        Sbf = [sb("Sbf0", [128, 4, 256], BF16), None]
        cst_sb = sb("cst_sb", [128, 6, 128])
        identb = sb("identb", [128, 128], BF16)
        onesD = sb("onesD", [128, 128], BF16)
        ones256 = sb("ones256", [128, 128], BF16)
        ones1 = sb("ones1", [1, 128], BF16)
        gcols = sb("gcols", [128, 88])
        tab_bc = sb("tab_bc", [128, 128])
        gbc = sb("gbc", [128, 512])
        neglam = sb("neglam", [128, 4])
        wg2b = sb("wg2b", [16, 512], BF16)
        bgb = sb("bgb", [1, 512], BF16)
        BIAS = sb("BIAS", [128, 4, 2, 128])
        sqb0 = sb("sqb0", [128, 512], BF16)
        sqb = [sqb0, sqb0]
        rstd = sb("rstd", [128, 512])
        sgt0 = sb("sgt0", [128, 512])
        sgt = [sgt0, sgt0]
        banks = [st.enter_context(nc.psum_tensor("bank%d" % i, [128, 512], F32)) for i in range(8)]
        epsc = sb("epsc", [128, 1])
        bar_scr = sb("bar_scr", [128, 2])
        P.pool(lambda e: e.memset(epsc[:], EPS), writes=["epsc"])

        ident = cst_sb[:, 0, :]
        CM = cst_sb[:, 1, :]
        TRI = cst_sb[:, 2, :]
        TRIR = cst_sb[:, 3, :]

        class Arena:
            def __init__(self):
                self.off = 0
                self.limit = AR_COLS

            def reset(self):
                self.off = 0

            def take(self, cols_bf16):
                o = self.off
                self.off += cols_bf16
                assert self.off <= self.limit, ("arena overflow", self.off, self.limit)
                return arena[:, o:o + cols_bf16]

            def bf(self, shape):
                n = int(np.prod(shape[1:]))
                v = self.take(n)
                return v if len(shape) == 2 else v.rearrange(_pat(len(shape)), **_dims(shape))

            def f32(self, shape):
                n = int(np.prod(shape[1:]))
                v = self.take(2 * n).bitcast(F32)
                return v if len(shape) == 2 else v.rearrange(_pat(len(shape)), **_dims(shape))

        def _pat(nd):
            names = "abcd"[:nd - 1]
            return "p (%s) -> p %s" % (" ".join(names), " ".join(names))

        def _dims(shape):
            names = "abcd"[:len(shape) - 1]
            return {n: s for n, s in zip(names[:-1], shape[1:-1])}

        AR = Arena()
        phase = [0]

        def arena_phase():
            P.act(lambda e: e.copy(out=bar_scr[:, 0:1], in_=epsc[:, 0:1]), reads=["epsc"], writes=["AR"])
            AR.reset()

        def CP(name):
            if stop == name:
                raise _StopBuild()

        uid = [0]

        def U(prefix):
            uid[0] += 1
            return "%s#%d" % (prefix, uid[0])

        slot_i = [0]
        wpass = {"mode": "once", "k": 0}
        SCR_COLS = 16 * 512
        scr_regions = {}
        pending_wb = []

        def scr_ap(k):
            return wscr[:, k * SCR_COLS:(k + 1) * SCR_COLS]

        def flush_wb(keep=0):
            while len(pending_wb) > keep:
                pending_wb.pop(0)()

        def after_load(i, sl, keys):
            if wpass["mode"] != "first":
                return
            k = wpass["k"]
            if k % wpass.get("nt", 1) != wpass.get("ti", 0):
                flush_wb()
                return
            pending_wb.append(lambda: P.dma(
                "pool", "wb%d" % (k % 2),
                lambda e: e.dma_start(out=scr_ap(k), in_=sl[:].rearrange("p a b -> p (a b)")),
                reads=list(keys), writes=["scr%d" % k]))
            flush_wb(keep=1)

        def load_from_scr(i, sl, keys):
            k = wpass["k"]
            P.dma("pool", "w%d" % i,
                  lambda e: e.dma_start(out=sl[:].rearrange("p a b -> p (a b)"), in_=scr_ap(k)),
                  reads=["scr%d" % k], writes=list(keys))

        def load_w(src_ap, nk, ncols):
            i = slot_i[0] % NSLOT
            slot_i[0] += 1
            s = slots[i]
            key = "slot%d" % i
            if wpass["mode"] == "later":
                load_from_scr(i, s, [key, key + "u"])
            else:
                P.dma("pool", "w%d" % i,
                      lambda e: e.dma_start(out=s[:, 0:nk, 0:ncols], in_=src_ap.rearrange("(k p) c -> p k c", p=128)),
                      writes=[key, key + "u"])
                after_load(i, s, [key, key + "u"])
            wpass["k"] += 1
            return s, key

        out_keys = []
        oq = [0]

        def store(dst_ap, src_ap, reads):
            k = U("out")
            out_keys.append(k)
            oq[0] += 1
            P.dma("sp", "st%d" % (oq[0] % 4), lambda e: e.dma_start(out=dst_ap, in_=src_ap), reads=reads, writes=[k])

        lq = [0]

        def load(dst_ap, src_ap, writes, reads=(), queue="sp", **kw):
            lq[0] += 1
            P.dma(queue, "ld%d" % (lq[0] % 4), lambda e: e.dma_start(out=dst_ap, in_=src_ap, **kw), reads=reads,
                  writes=writes)

        if stop == "pre":
            P.add("sp", lambda e: e.nop(), reads=[])
            stats = P.emit(nc, st)
            return nc, stats
        load(cst_sb[:].rearrange("p a b -> p (a b)"), cst, ["cst"])
        load(tab_bc[:], table.partition_broadcast(128), ["tab"])
        gb_src = [(dqn, 0, 64), (dkn, 64, 64), (mqn, 128, 128), (mkn, 256, 128), (don, 384, 128)]
        for (src, o, n) in gb_src:
            load(gbc[:, o:o + n], src.partition_broadcast(128), ["gbc"])
        AR.reset()
        grow = AR.f32([128, 128])
        dl = AR.f32([128, 256])
        dlp = AR.f32([128, 128])
        P.pool(lambda e: e.memset(grow, 0.0), reads=["AR"], writes=["grow"])
        load(grow[0:80, :], nvec.rearrange("v (c p) -> (v c) p", p=128), ["grow"], reads=["AR"])
        load(grow[80:82, :], glan.rearrange("(c p) -> c p", p=128), ["grow"], reads=["AR"])
        load(dl, dlam.partition_broadcast(128), ["dl"], reads=["AR"])
        P.dma("pool", "wq", lambda e: e.dma_start(out=wg2b[:], in_=wg2), writes=["wg2b"])
        P.dma("pool", "wq", lambda e: e.dma_start(out=bgb[:], in_=bg), writes=["bgb"])
        P.pool(lambda e: e.memset(onesD[:], 1.0 / D), writes=["onesD"])
        P.pool(lambda e: e.memset(ones256[:], 1.0 / 256), writes=["ones256"])
        P.pool(lambda e: e.memset(ones1[:], 1.0), writes=["ones1"])
        P.dve(lambda e: e.memset(vaug[:].rearrange("p a b c -> p (a b c)"), 1.0), writes=["vaug_init"])
        P.dve(lambda e: e.memset(vnew[:].rearrange("p a b c -> p (a b c)"), 1.0), writes=["vnew_init"])
        P.dve(lambda e: e.memset(mvaug[0][:].rearrange("p a b c -> p (a b c)"), 1.0), writes=["mvaug0"])
        P.dve(lambda e: e.tensor_copy(out=identb[:], in_=ident), reads=["cst"], writes=["identb"])
        P.pe(lambda e: e.transpose(out=banks[0][:, 0:128], in_=grow, identity=ident), reads=["grow", "cst", "AR"],
             writes=["bank0"])
        P.dve(lambda e: e.tensor_copy(out=gcols[:], in_=banks[0][:, 0:88]), reads=["bank0"], writes=["gcols"])
        P.dve(lambda e: e.tensor_tensor(out=dlp[:, 0:64], in0=dl[:, 0:64], in1=dl[:, 64:128], op=ALU.mult),
              reads=["dl", "AR"], writes=["dlp"])
        P.dve(lambda e: e.tensor_tensor(out=dlp[:, 64:128], in0=dl[:, 128:192], in1=dl[:, 192:256], op=ALU.mult),
              reads=["dl", "AR"], writes=["dlp"])
        P.dve(lambda e: e.tensor_reduce(out=neglam[:, 0:2], in_=dlp.rearrange("p (a b) -> p a b", a=2), axis=AX.X,
                                        op=ALU.add), reads=["dlp", "AR"], writes=["nl0"])
        P.act(lambda e: e.activation(out=neglam[:, 2:4], in_=neglam[:, 0:2], func=AF.Exp), reads=["nl0"], writes=["nl1"])
        P.dve(lambda e: e.tensor_tensor(out=neglam[:, 0:1], in0=neglam[:, 3:4], in1=neglam[:, 2:3], op=ALU.subtract),
              reads=["nl1"], writes=["nl2"])
        P.dve(lambda e: e.tensor_scalar(out=neglam[:, 0:1], in0=neglam[:, 0:1], scalar1=-LAM_INIT, scalar2=None,
                                        op0=ALU.add), reads=["nl2"], writes=["neglam"])
        P.dve(lambda e: e.tensor_scalar(out=gbc[:, 384:512], in0=gbc[:, 384:512], scalar1=1.0 - LAM_INIT, scalar2=None,
                                        op0=ALU.mult), reads=["gbc"], writes=["gbc"])
        deferred = []
        btmp = rstd[:, 0:128]

        def _bias_ops():
            for ty in range(2):
                BK = cst_sb[:, 4 + ty, :]
                brange = range(0, 32) if ty == 0 else range(1, 16)
                for h in range(4):
                    dst = BIAS[:, h, ty, :]
                    first = True
                    for b in brange:
                        col = b * 4 + h
                        if first:
                            deferred.append(lambda dst=dst, BK=BK, b=b, col=col: P.dve(
                                lambda e: e.tensor_scalar(out=dst, in0=BK, scalar1=float(b),
                                                          scalar2=tab_bc[:, col:col + 1], op0=ALU.is_equal,
                                                          op1=ALU.mult), reads=["cst", "tab"], writes=["BIAS"]))
                            first = False
                        else:
                            deferred.append(lambda BK=BK, b=b, col=col: P.dve(
                                lambda e: e.tensor_scalar(out=btmp, in0=BK, scalar1=float(b),
                                                          scalar2=tab_bc[:, col:col + 1], op0=ALU.is_equal,
                                                          op1=ALU.mult), reads=["cst", "tab"], writes=["rstd"]))
                            deferred.append(lambda dst=dst: P.dve(
                                lambda e: e.tensor_tensor(out=dst, in0=dst, in1=btmp, op=ALU.add),
                                reads=["rstd", "BIAS"], writes=["BIAS"]))
            for h in range(4):
                deferred.append(lambda h=h: P.dve(
                    lambda e: e.tensor_scalar(out=BIAS[:, h, :, :], in0=BIAS[:, h, :, :],
                                              scalar1=tab_bc[:, 60 + h:61 + h], scalar2=None, op0=ALU.subtract),
                    reads=["BIAS", "tab"], writes=["BIAS"]))
            deferred.append(lambda: P.dve(lambda e: e.memset(BIAS[64:128, :, 0, 0:64], -30000.0), reads=["BIAS"],
                                          writes=["BIAS"]))

        _bias_ops()

        def sprinkle(k):
            for _ in range(min(k, len(deferred))):
                deferred.pop(0)()

        def norm_F(TW, gi):
            bk = banks[7]
            for c in range(KC):
                s = sqb[c % 2]
                P.act(lambda e, c=c, s=s: e.activation(out=s[:, 0:TW], in_=xT[:, c, 0:TW], func=AF.Square),
                      reads=["xT%d" % c], writes=["sqb0"])
                P.pe(lambda e, c=c, s=s: e.matmul(bk[:, 0:TW], lhsT=onesD[:], rhs=s[:, 0:TW], start=(c == 0),
                                                  stop=(c == KC - 1)),
                     reads=["onesD", "sqb0"], writes=["bank7"])
            P.act(lambda e: e.activation(out=rstd[:, 0:TW], in_=bk[:, 0:TW], func=AF.Ln, bias=epsc[:, 0:1]),
                  reads=["bank7", "epsc"], writes=["rstd"])
            P.act(lambda e: e.activation(out=rstd[:, 0:TW], in_=rstd[:, 0:TW], func=AF.Exp, scale=-0.5),
                  reads=["rstd"], writes=["rstd"])
            for c in range(KC):
                P.dve(lambda e, c=c: e.scalar_tensor_tensor(out=hT[:, c, 0:TW], in0=xT[:, c, 0:TW],
                                                            scalar=gcols[:, gi * 16 + c:gi * 16 + c + 1],
                                                            in1=rstd[:, 0:TW], op0=ALU.mult, op1=ALU.mult),
                      reads=["xT%d" % c, "gcols", "rstd"], writes=["hT%d" % c])


        def load_x_tile(src, subs):
            AR.reset()
            stg = [AR.f32([128, D]) for _ in range(2)]
            for si, (row0, col0, n) in enumerate(subs):
                sg = stg[si % 2]
                sk = "stg%d" % (si % 2)
                load(sg[0:n, :], src[row0:row0 + n, :], [sk], reads=["AR"])
                for q in range(4):
                    bk = banks[q % 4]
                    for j in range(4):
                        c = q * 4 + j
                        P.pe(lambda e, sg=sg, c=c, j=j, bk=bk, n=n: e.transpose(
                            out=bk[:, j * 128:j * 128 + n], in_=sg[0:n, c * 128:(c + 1) * 128], identity=ident[0:n, 0:n]),
                            reads=[sk, "cst", "AR"], writes=["bank%d" % (q % 4)])
                    eng = P.act if q % 2 == 0 else P.dve
                    if q % 2 == 0:
                        P.act(lambda e, q=q, bk=bk, n=n, col0=col0: e.copy(
                            out=xT[:, q * 4:q * 4 + 4, col0:col0 + n],
                            in_=bk[:].rearrange("p (a b) -> p a b", a=4)[:, :, 0:n]),
                            reads=["bank%d" % (q % 4)], writes=["xT%d" % (q * 4 + j) for j in range(4)])
                    else:
                        P.dve(lambda e, q=q, bk=bk, n=n, col0=col0: e.tensor_copy(
                            out=xT[:, q * 4:q * 4 + 4, col0:col0 + n],
                            in_=bk[:].rearrange("p (a b) -> p a b", a=4)[:, :, 0:n]),
                            reads=["bank%d" % (q % 4)], writes=["xT%d" % (q * 4 + j) for j in range(4)])

        def ffn(TW, wi, wo_, gi):
            norm_F(TW, gi)
            arena_phase()
            actT = AR.bf([128, NFF, TW])
            bi = 0
            for c0 in range(0, FF, 256):
                ncol = min(256, FF - c0)
                i = slot_i[0] % NSLOT
                slot_i[0] += 1
                sg_ = slots[i]
                kg = "slot%d" % i
                if wpass["mode"] == "later":
                    load_from_scr(i, sg_, [kg, kg + "u"])
                else:
                    P.dma("pool", "w%d" % i, lambda e, sg_=sg_, c0=c0, ncol=ncol: e.dma_start(
                        out=sg_[:, :, 0:ncol], in_=wi[:, c0:c0 + ncol].rearrange("(k p) c -> p k c", p=128)), writes=[kg])
                    P.dma("pool", "wu%d" % i, lambda e, sg_=sg_, c0=c0, ncol=ncol: e.dma_start(
                        out=sg_[:, :, 256:256 + ncol],
                        in_=wi[:, FF + c0:FF + c0 + ncol].rearrange("(k p) c -> p k c", p=128)), writes=[kg + "u"])
                    after_load(i, sg_, [kg, kg + "u"])
                wpass["k"] += 1
                for j in range(ncol // 128):
                    f = c0 // 128 + j
                    bG = banks[(bi % 2) * 2]
                    bU = banks[(bi % 2) * 2 + 1]
                    kG = "bank%d" % ((bi % 2) * 2)
                    kU = "bank%d" % ((bi % 2) * 2 + 1)
                    for kc in range(KC):
                        P.pe(lambda e, kc=kc, j=j, bG=bG, sg_=sg_: e.matmul(
                            bG[:, 0:TW], lhsT=sg_[:, kc, j * 128:(j + 1) * 128], rhs=hT[:, kc, 0:TW], start=(kc == 0),
                            stop=(kc == KC - 1)), reads=[kg, "hT%d" % kc], writes=[kG])
                    for kc in range(KC):
                        P.pe(lambda e, kc=kc, j=j, bU=bU, sg_=sg_: e.matmul(
                            bU[:, 0:TW], lhsT=sg_[:, kc, 256 + j * 128:256 + (j + 1) * 128], rhs=hT[:, kc, 0:TW],
                            start=(kc == 0), stop=(kc == KC - 1)), reads=[kg + "u", "hT%d" % kc], writes=[kU])
                    t = sgt[bi % 2]
                    P.act(lambda e, t=t, bG=bG: e.activation(out=t[:, 0:TW], in_=bG[:, 0:TW], func=AF.Silu),
                          reads=[kG], writes=["sgt0"])
                    P.dve(lambda e, t=t, bU=bU, f=f: e.tensor_tensor(out=actT[:, f, 0:TW], in0=t[:, 0:TW],
                                                                    in1=bU[:, 0:TW], op=ALU.mult),
                          reads=["sgt0", kU, "AR"], writes=["actT%d" % f])
                    bi += 1
                    sprinkle(10)
            CP("ffn_in")
            for slab in range(4):
                for (f0, nf) in ((0, 16), (16, 16), (32, 11)):
                    s_, ks = load_w(wo_[f0 * 128:(f0 + nf) * 128, slab * 512:(slab + 1) * 512], nf, 512)
                    for ff in range(f0, f0 + nf):
                        for dc in range(4):
                            P.pe(lambda e, ff=ff, dc=dc, s_=s_, f0=f0: e.matmul(
                                banks[4 + dc][:, 0:TW], lhsT=s_[:, ff - f0, dc * 128:(dc + 1) * 128],
                                rhs=actT[:, ff, 0:TW], start=(ff == 0), stop=(ff == NFF - 1)),
                                reads=[ks, "actT%d" % ff, "AR"], writes=["bank%d" % (4 + dc)])
                for dc in range(4):
                    c = slab * 4 + dc
                    P.dve(lambda e, c=c, dc=dc: e.scalar_tensor_tensor(
                        out=xT[:, c, 0:TW], in0=banks[4 + dc][:, 0:TW], scalar=0.5, in1=xT[:, c, 0:TW], op0=ALU.mult,
                        op1=ALU.add), reads=["bank%d" % (4 + dc), "xT%d" % c], writes=["xT%d" % c])

        def store_y_tile(dst, subs, TW, gi):
            norm_F_out(TW, gi, dst, subs)

        def norm_F_out(TW, gi, dst, subs):
            bk = banks[7]
            for c in range(KC):
                s = sqb[c % 2]
                P.act(lambda e, c=c, s=s: e.activation(out=s[:, 0:TW], in_=xT[:, c, 0:TW], func=AF.Square),
                      reads=["xT%d" % c], writes=["sqb0"])
                P.pe(lambda e, c=c, s=s: e.matmul(bk[:, 0:TW], lhsT=onesD[:], rhs=s[:, 0:TW], start=(c == 0),
                                                  stop=(c == KC - 1)),
                     reads=["onesD", "sqb0"], writes=["bank7"])
            P.act(lambda e: e.activation(out=rstd[:, 0:TW], in_=bk[:, 0:TW], func=AF.Ln, bias=epsc[:, 0:1]),
                  reads=["bank7", "epsc"], writes=["rstd"])
            P.act(lambda e: e.activation(out=rstd[:, 0:TW], in_=rstd[:, 0:TW], func=AF.Exp, scale=-0.5),
                  reads=["rstd"], writes=["rstd"])
            for c in range(KC):
                P.dve(lambda e, c=c: e.scalar_tensor_tensor(out=xT[:, c, 0:TW], in0=xT[:, c, 0:TW],
                                                            scalar=gcols[:, gi * 16 + c:gi * 16 + c + 1],
                                                            in1=rstd[:, 0:TW], op0=ALU.mult, op1=ALU.mult),
                      reads=["xT%d" % c, "gcols", "rstd"], writes=["xT%d" % c])
            arena_phase()
            stg = [AR.f32([128, D]) for _ in range(2)]
            for si, (row0, col0, n) in enumerate(subs):
                sg = stg[si % 2]
                sk = "ostg%d" % (si % 2)
                for q in range(4):
                    bk2 = banks[q % 4]
                    for j in range(4):
                        c = q * 4 + j
                        P.pe(lambda e, c=c, j=j, bk2=bk2, n=n, col0=col0: e.transpose(
                            out=bk2[0:n, j * 128:(j + 1) * 128], in_=xT[:, c, col0:col0 + n], identity=ident),
                            reads=["xT%d" % c, "cst"], writes=["bank%d" % (q % 4)])
                    if q % 2 == 0:
                        P.act(lambda e, q=q, bk2=bk2, n=n, sg=sg: e.copy(out=sg[0:n, q * 512:(q + 1) * 512],
                                                                        in_=bk2[0:n, :]),
                              reads=["bank%d" % (q % 4), "AR"], writes=[sk])
                    else:
                        P.dve(lambda e, q=q, bk2=bk2, n=n, sg=sg: e.tensor_copy(out=sg[0:n, q * 512:(q + 1) * 512],
                                                                               in_=bk2[0:n, :]),
                              reads=["bank%d" % (q % 4), "AR"], writes=[sk])
                store(dst[row0:row0 + n, :], sg[0:n, :], [sk, "AR"])

        def proj_T(col0, n, s_, ks, ncols, bk, bkey, c_off=0):
            for kc in range(KC):
                P.pe(lambda e, kc=kc: e.matmul(bk[0:n, 0:ncols], lhsT=hT[:, kc, col0:col0 + n],
                                               rhs=s_[:, kc, c_off:c_off + ncols], start=(kc == 0), stop=(kc == KC - 1)),
                     reads=[ks, "hT%d" % kc], writes=[bkey])

        def groupnorm_T(src_key, src, n, G, gs, gain, dst_f32, dst_bf, tmp, ss, keys_w, extra_reads=()):
            W_ = G * gs
            P.dve(lambda e: e.tensor_tensor(out=tmp[0:n, 0:W_], in0=src, in1=src, op=ALU.mult),
                  reads=[src_key, "AR"], writes=["gn_tmp"])
            P.dve(lambda e: e.tensor_reduce(out=ss[0:n, 0:G], in_=tmp[0:n, 0:W_].rearrange("p (a b) -> p a b", a=G),
                                            axis=AX.X, op=ALU.add), reads=["gn_tmp", "AR"], writes=["gn_ss"])
            P.act(lambda e: e.activation(out=ss[0:n, G:2 * G], in_=ss[0:n, 0:G], func=AF.Ln, bias=epsc[0:n, 0:1],
                                         scale=1.0 / gs), reads=["gn_ss", "epsc", "AR"], writes=["gn_ss2"])
            P.act(lambda e: e.activation(out=ss[0:n, G:2 * G], in_=ss[0:n, G:2 * G], func=AF.Exp, scale=-0.5),
                  reads=["gn_ss2", "AR"], writes=["gn_ss2"])
            P.dve(lambda e: e.tensor_tensor(out=tmp[0:n, 0:W_].rearrange("p (a b) -> p a b", a=G),
                                            in0=src.rearrange("p (a b) -> p a b", a=G),
                                            in1=ss[0:n, G:2 * G].unsqueeze(2).broadcast_to([n, G, gs]), op=ALU.mult),
                  reads=[src_key, "gn_ss2", "AR"], writes=["gn_tmp"])
            outs = []
            if dst_f32 is not None:
                outs.append(dst_f32)
            if dst_bf is not None:
                outs.append(dst_bf)
            for di, dst in enumerate(outs):
                P.dve(lambda e, dst=dst: e.tensor_tensor(out=dst.rearrange("p (a b) -> p a b", a=G),
                                                        in0=tmp[0:n, 0:W_].rearrange("p (a b) -> p a b", a=G),
                                                        in1=gain[0:n, :].unsqueeze(1).broadcast_to([n, G, gs]),
                                                        op=ALU.mult),
                      reads=["gn_tmp", "gbc", "AR"] + list(extra_reads), writes=[keys_w[di]])

        def transpose_T2F(src_bf, src_key, n, nblk, dst_fn, dst_keys, bank_i):
            bk = banks[bank_i].bitcast(BF16)
            for j in range(nblk):
                P.pe(lambda e, j=j: e.transpose(out=bk[:, j * 128:j * 128 + n], in_=src_bf[0:n, j * 128:(j + 1) * 128],
                                                identity=identb[0:n, 0:n]),
                     reads=[src_key, "identb", "AR"], writes=["bank%d" % bank_i])
            for j in range(nblk):
                if True:
                    P.act(lambda e, j=j: e.copy(out=dst_fn(j), in_=bk[:, j * 128:j * 128 + n]),
                          reads=["bank%d" % bank_i, "AR"], writes=[dst_keys[j]])
                else:
                    P.dve(lambda e, j=j: e.tensor_copy(out=dst_fn(j), in_=bk[:, j * 128:j * 128 + n]),
                          reads=["bank%d" % bank_i, "AR"], writes=[dst_keys[j]])

        def mixer(TW, subs, kind, tile_i):
            sprinkle(10 ** 6)
            norm_F(TW, 1)
            arena_phase()
            NS = len(subs)
            catB = AR.bf([128, 8, TW])
            tproc = AR.f32([128, 512])
            gtmp = AR.f32([128, 512])
            gss_ = AR.f32([128, 16])
            oute = [AR.f32([128, 512]) for _ in range(2)]
            tb = AR.bf([128, 512])
            dqT = AR.bf([128, 4, TW])
            Ebuf = [AR.bf([128, 4, 128]) for _ in range(2)]
            stmp = AR.f32([128, 128])
            dacc = AR.f32([128, 512])
            t1 = AR.f32([128, 128])
            rc = AR.f32([128, 4])
            Enear = [AR.bf([128, 2, 128]) for _ in range(2)]
            stmp2 = [AR.f32([128, 128]) for _ in range(4)]
            en = [0]

            def out_rows(sub):
                return sub["orow"]

            s_, ks = load_w(w_in[:, C_DK:C_DK + 512], 16, 512)
            for si, sub in enumerate(subs):
                proj_T(sub["col0"], sub["n"], s_, ks, 512, banks[si % 4], "bank%d" % (si % 4))
            for si, sub in enumerate(subs):
                n, col0 = sub["n"], sub["col0"]
                bk, bkey = banks[si % 4], "bank%d" % (si % 4)
                P.act(lambda e, bk=bk, n=n: e.copy(out=tproc[0:n, :], in_=bk[0:n, :]), reads=[bkey, "AR"],
                      writes=["tproc"])
                CP("m_dk_a")
                oe = oute[si % 2]
                oek = "oute%d" % (si % 2)
                groupnorm_T("tproc", tproc[0:n, :], n, 8, 64, gbc[:, 64:128], oe[0:n, :], tb[0:n, :], gtmp, gss_,
                            [oek, "tb"])
                CP("m_dk_b")
                store(sub["dk_dst"], oe[0:n, :], [oek, "AR"])
                CP("m_dk_c")
                if kind == "prompt":
                    p0 = sub["pos0"]
                    transpose_T2F(tb, "tb", n, 4, lambda j, p0=p0, n=n: dkT[:, j, p0:p0 + n],
                                  ["dkT"] * 4, 4 + si % 2)
                    CP("m_dk_d")
                else:
                    sq_ = sub["seq"]
                    transpose_T2F(tb, "tb", n, 4, lambda j, sq_=sq_, n=n: dkTn[:, j, sq_ * 32:sq_ * 32 + n],
                                  ["dkTn"] * 4, 4 + si % 2)
            CP("m_dk")
            s_, ks = load_w(w_in[:, C_DV:C_DV + 512], 16, 512)
            for si, sub in enumerate(subs):
                n, col0 = sub["n"], sub["col0"]
                bk, bkey = banks[si % 4], "bank%d" % (si % 4)
                proj_T(col0, n, s_, ks, 512, bk, bkey)
                oe = oute[si % 2]
                oek = "oute%d" % (si % 2)
                P.act(lambda e, bk=bk, n=n, oe=oe: e.copy(out=oe[0:n, :], in_=bk[0:n, :]), reads=[bkey, "AR"],
                      writes=[oek])
                store(sub["dv_dst"], oe[0:n, :], [oek, "AR"])
                CP("m_dv_a")
                if kind == "prompt":
                    blk = sub["pos0"] // 128
                    P.dve(lambda e, oe=oe, n=n, blk=blk: e.tensor_copy(
                        out=vaug[0:n, blk, :, 0:128], in_=oe[0:n, :].rearrange("p (a b) -> p a b", a=4)),
                        reads=[oek, "vaug_init", "AR"], writes=["vaug"])
                    CP("m_dv_b")
                else:
                    sq_ = sub["seq"]
                    P.dve(lambda e, oe=oe, n=n, sq_=sq_: e.tensor_copy(
                        out=vnew[0:n, sq_, :, 0:128], in_=oe[0:n, :].rearrange("p (a b) -> p a b", a=4)),
                        reads=[oek, "vnew_init", "AR"], writes=["vnew"])

            def qproj(c_w, gain, G, gs):
                s2, ks2 = load_w(w_in[:, c_w:c_w + 512], 16, 512)
                for si, sub in enumerate(subs):
                    proj_T(sub["col0"], sub["n"], s2, ks2, 512, banks[si % 4], "bank%d" % (si % 4))
                for si, sub in enumerate(subs):
                    n, col0 = sub["n"], sub["col0"]
                    bk, bkey = banks[si % 4], "bank%d" % (si % 4)
                    P.act(lambda e, bk=bk, n=n: e.copy(out=tproc[0:n, :], in_=bk[0:n, :]), reads=[bkey, "AR"],
                          writes=["tproc"])
                    groupnorm_T("tproc", tproc[0:n, :], n, G, gs, gain, None, tb[0:n, :], gtmp, gss_, ["tb"])
                    transpose_T2F(tb, "tb", n, 4, lambda j, col0=col0, n=n: dqT[:, j, col0:col0 + n],
                                  ["dqT"] * 4, 4 + si % 2)

            def finish_cat(n, col0, gain, normed, cbase):
                if normed:
                    groupnorm_T("dacc", dacc[0:n, :], n, 4, 128, gain, None, tb[0:n, :], gtmp, gss_, ["tb"])
                    CP("fc_a")
                else:
                    P.act(lambda e: e.copy(out=tb[0:n, :], in_=dacc[0:n, :]), reads=["dacc", "AR"], writes=["tb"])
                transpose_T2F(tb, "tb", n, 4, lambda j: catB[:, cbase + j, col0:col0 + n],
                              ["catB%d" % (cbase + j) for j in range(4)], 5)

            CP("m_dv")
            qproj(C_DQ, gbc[:, 0:64], 8, 64)
            CP("m_dq")
            ei = [0]

            class Pipe:
                def __init__(self):
                    self.pending = None

                def push(self, s1, s2):
                    s1()
                    if self.pending is not None:
                        self.pending()
                    self.pending = s2

                def flush(self):
                    if self.pending is not None:
                        self.pending()
                    self.pending = None

            pipe = Pipe()

            def diff_subtile(si, sub):
                n, col0 = sub["n"], sub["col0"]
                blocks = []
                if kind == "prompt":
                    qb = sub["pos0"] // 128
                    for b in range(qb + 1):
                        ty = 0 if b == qb else (1 if b == qb - 1 else 2)
                        blocks.append((("p", b), 128, ty))
                else:
                    sq_ = sub["seq"]
                    for b in range(8):
                        blocks.append((("c", sq_, b), 128, 1 if b == 7 else 2))
                    blocks.append((("n", sq_), n, 0))

                def kT_ap(bd, h, c):
                    if bd[0] == "p":
                        return dkT[c * 64:(c + 1) * 64, h, bd[1] * 128:(bd[1] + 1) * 128]
                    if bd[0] == "c":
                        o = bd[1] * 1024 + bd[2] * 128
                        return dkT[c * 64:(c + 1) * 64, h, o:o + 128]
                    return dkTn[c * 64:(c + 1) * 64, h, bd[1] * 32:bd[1] * 32 + n]

                def v_ap(bd, h):
                    if bd[0] == "p":
                        return vaug[:, bd[1], h, 0:129]
                    if bd[0] == "c":
                        return vaug[:, bd[1] * 8 + bd[2], h, 0:129]
                    return vnew[0:n, bd[1], h, 0:129]

                far_blocks = [b_ for b_ in blocks if b_[2] == 2]
                near_blocks = [b_ for b_ in blocks if b_[2] != 2]
                nblk_all = len(blocks)

                def finalize_head(h, ob, obk):
                    P.dve(lambda e: e.reciprocal(out=rc[0:n, 0:2],
                                                 in_=ob[0:n, :].rearrange("p (a b) -> p a b", a=2)[:, :, 128]),
                          reads=[obk, "AR"], writes=["rc"])
                    P.dve(lambda e: e.tensor_tensor(out=rc[0:n, 2:3], in0=rc[0:n, 1:2], in1=neglam[0:n, 0:1], op=ALU.mult),
                          reads=["rc", "neglam", "AR"], writes=["rc2"])
                    P.act(lambda e: e.activation(out=t1[0:n, :], in_=ob[0:n, 0:128], func=AF.Copy,
                                                 scale=rc[0:n, 0:1]), reads=[obk, "rc", "AR"], writes=["t1"])
                    P.dve(lambda e: e.scalar_tensor_tensor(
                        out=dacc[0:n, h * 128:(h + 1) * 128], in0=ob[0:n, 256:384], scalar=rc[0:n, 2:3], in1=t1[0:n, :],
                        op0=ALU.mult, op1=ALU.add), reads=[obk, "rc2", "t1", "AR"], writes=["dacc"])

                def make_item(h, c, grp, cnt0, is_far, last_of_head, last_of_sub):
                    ob = banks[2 + h % 2]
                    obk = "bank%d" % (2 + h % 2)
                    if is_far:
                        sbk = banks[ei[0] % 2]
                        sbkey = "bank%d" % (ei[0] % 2)
                        E = Ebuf[ei[0] % 2]
                        Ek = "E%d" % (ei[0] % 2)
                        ei[0] += 1
                    else:
                        sbk = banks[6 + en[0] % 2]
                        sbkey = "bank%d" % (6 + en[0] % 2)
                        E = Enear[en[0] % 2]
                        Ek = "En%d" % (en[0] % 2)
                        eni = en[0] % 2
                        en[0] += 1
                    ng = len(grp)

                    def s1():
                        for j, (bd, kn, ty) in enumerate(grp):
                            P.pe(lambda e, bd=bd, kn=kn, j=j: e.matmul(
                                sbk[0:kn, j * 128:j * 128 + n], lhsT=kT_ap(bd, h, c),
                                rhs=dqT[c * 64:(c + 1) * 64, h, col0:col0 + n], start=True, stop=True),
                                reads=["dkT", "dkTn", "dqT", "AR"], writes=[sbkey])
                        if is_far:
                            P.act(lambda e: e.activation(
                                out=E[:, 0:ng, 0:n], in_=sbk[:].rearrange("p (a b) -> p a b", a=4)[:, 0:ng, 0:n],
                                func=AF.Exp, scale=0.125), reads=[sbkey, "AR"], writes=[Ek])
                        else:
                            for j, (bd, kn, ty) in enumerate(grp):
                                stj = stmp2[(eni * 2 + j) % 4]
                                stk = "stmp%d" % ((eni * 2 + j) % 4)
                                P.dve(lambda e, j=j, kn=kn, ty=ty, stj=stj: e.scalar_tensor_tensor(
                                    out=stj[0:kn, 0:n], in0=sbk[0:kn, j * 128:j * 128 + n], scalar=0.125,
                                    in1=BIAS[0:kn, h, ty, 0:n], op0=ALU.mult, op1=ALU.add),
                                    reads=[sbkey, "BIAS", "AR"], writes=[stk])
                                P.act(lambda e, j=j, kn=kn, stj=stj: e.activation(out=E[0:kn, j, 0:n], in_=stj[0:kn, 0:n],
                                                                                  func=AF.Exp),
                                      reads=[stk, "AR"], writes=[Ek])

                    def s2():
                        for j, (bd, kn, ty) in enumerate(grp):
                            cnt = cnt0 + j
                            P.pe(lambda e, bd=bd, kn=kn, j=j, cnt=cnt: e.matmul(
                                ob[0:n, c * 256:c * 256 + 129], lhsT=E[0:kn, j, 0:n], rhs=v_ap(bd, h),
                                start=(cnt == 0), stop=(cnt == nblk_all - 1)),
                                reads=[Ek, "vaug", "vnew", "AR"], writes=[obk])
                        if last_of_head:
                            finalize_head(h, ob, obk)
                        if last_of_sub:
                            finish_cat(n, col0, gbc[:, 384:512], True, 0)
                            CP("dsi%d" % si)
                    return s1, s2

                for h in range(4):
                    for c in range(2):
                        cnt = 0
                        for g0 in range(0, len(far_blocks), 4):
                            grp = far_blocks[g0:g0 + 4]
                            pipe.push(*make_item(h, c, grp, cnt, True, False, False))
                            cnt += len(grp)
                        pipe.push(*make_item(h, c, near_blocks, cnt, False, c == 1, (c == 1 and h == 3)))

            for si, sub in enumerate(subs):
                diff_subtile(si, sub)
            pipe.flush()

            CP("m_diff")
            qproj(C_MQ, gbc[:, 128:256], 4, 128)

            def mem_item(si, sub, h):
                n, col0 = sub["n"], sub["col0"]
                mi = sub["mem"]
                ob = banks[2 + h % 2]
                obk = "bank%d" % (2 + h % 2)
                sbk = banks[ei[0] % 2]
                sbkey = "bank%d" % (ei[0] % 2)
                E = Ebuf[ei[0] % 2]
                Ek = "E%d" % (ei[0] % 2)
                ei[0] += 1

                def s1():
                    for mb in range(2):
                        P.pe(lambda e, mb=mb: e.matmul(
                            sbk[:, mb * 128:mb * 128 + n], lhsT=mkT[mi][:, h, mb * 128:(mb + 1) * 128],
                            rhs=dqT[:, h, col0:col0 + n], start=True, stop=True),
                            reads=["mkT%d" % mi, "dqT", "AR"], writes=[sbkey])
                    P.act(lambda e: e.activation(
                        out=E[:, 0:2, 0:n], in_=sbk[:].rearrange("p (a b) -> p a b", a=4)[:, 0:2, 0:n], func=AF.Exp,
                        scale=float(128 ** -0.5)), reads=[sbkey, "AR"], writes=[Ek])

                def s2():
                    for mb in range(2):
                        P.pe(lambda e, mb=mb: e.matmul(
                            ob[0:n, 0:129], lhsT=E[:, mb, 0:n], rhs=mvaug[mi][:, mb, h, 0:129], start=(mb == 0),
                            stop=(mb == 1)), reads=[Ek, "mvaug%d" % mi, "AR"], writes=[obk])
                    P.dve(lambda e: e.reciprocal(out=rc[0:n, 0:1], in_=ob[0:n, 128:129]), reads=[obk, "AR"],
                          writes=["rc"])
                    P.act(lambda e: e.activation(out=dacc[0:n, h * 128:(h + 1) * 128], in_=ob[0:n, 0:128],
                                                 func=AF.Copy, scale=rc[0:n, 0:1]),
                          reads=[obk, "rc", "AR"], writes=["dacc"])
                    if h == 3:
                        finish_cat(n, col0, None, False, 4)
                return s1, s2

            for si, sub in enumerate(subs):
                for h in range(4):
                    pipe.push(*mem_item(si, sub, h))
            pipe.flush()

            def apply_wo(cat, ckeys, row0):
                for slab in range(4):
                    s3, ks3 = load_w(wo[row0:row0 + 1024, slab * 512:(slab + 1) * 512], 8, 512)
                    for dc in range(4):
                        for cc in range(8):
                            P.pe(lambda e, dc=dc, cc=cc, s3=s3: e.matmul(
                                banks[4 + dc][:, 0:TW], lhsT=s3[:, cc, dc * 128:(dc + 1) * 128], rhs=cat[:, cc, 0:TW],
                                start=(cc == 0), stop=(cc == 7)), reads=[ks3, ckeys[cc], "AR"],
                                writes=["bank%d" % (4 + dc)])
                    for dc in range(4):
                        c = slab * 4 + dc
                        P.dve(lambda e, c=c, dc=dc: e.tensor_tensor(out=xT[:, c, 0:TW], in0=banks[4 + dc][:, 0:TW],
                                                                   in1=xT[:, c, 0:TW], op=ALU.add),
                              reads=["bank%d" % (4 + dc), "xT%d" % c], writes=["xT%d" % c])

            CP("m_mem")
            apply_wo(catB, ["catB%d" % j for j in range(8)], 1024)
            CP("m_woB")

            arena_phase()
            catA = AR.bf([128, 8, TW])
            gqT = AR.bf([128, 4, TW])
            gkT = AR.bf([128, 4, TW])
            gkTok = AR.bf([128, NS, 512])
            gvTok = AR.bf([128, NS, 1024])
            glrT = AR.bf([128, 512])[0:16, :]
            lbuf = AR.f32([128, 512])
            eb = AR.f32([128, 4, 128])
            enb = AR.f32([128, 4, 128])
            qe = AR.bf([128, 4, 128])
            ke = AR.bf([128, 4, 128])
            k2 = AR.bf([128, 512])
            AmT = AR.bf([128, 4, 128])
            rs2 = AR.f32([128, 128])

            for (c_w, dstT, dk_) in ((C_GQ, gqT, "gqT"), (C_GK, gkT, "gkT")):
                s4, ks4 = load_w(w_in[:, c_w:c_w + 512], 16, 512)
                for j in range(4):
                    bk, bkey = banks[j % 4], "bank%d" % (j % 4)
                    for kc in range(KC):
                        P.pe(lambda e, kc=kc, j=j, bk=bk, s4=s4: e.matmul(
                            bk[:, 0:TW], lhsT=s4[:, kc, j * 128:(j + 1) * 128], rhs=hT[:, kc, 0:TW], start=(kc == 0),
                            stop=(kc == KC - 1)), reads=[ks4, "hT%d" % kc], writes=[bkey])
                    P.act(lambda e, j=j, bk=bk, dstT=dstT: e.copy(out=dstT[:, j, 0:TW], in_=bk[:, 0:TW]),
                          reads=[bkey, "AR"], writes=[dk_])
                if c_w == C_GK:
                    for si, sub in enumerate(subs):
                        n, col0 = sub["n"], sub["col0"]
                        bk, bkey = banks[4 + si % 4], "bank%d" % (4 + si % 4)
                        proj_T(col0, n, s4, ks4, 512, bk, bkey)
                        P.act(lambda e, bk=bk, n=n, si=si: e.copy(out=gkTok[0:n, si, :], in_=bk[0:n, :]),
                              reads=[bkey, "AR"], writes=["gkTok"])
            for half in range(2):
                s5, ks5 = load_w(w_in[:, C_GV + half * 512:C_GV + (half + 1) * 512], 16, 512)
                for si, sub in enumerate(subs):
                    n, col0 = sub["n"], sub["col0"]
                    bk, bkey = banks[si % 4], "bank%d" % (si % 4)
                    proj_T(col0, n, s5, ks5, 512, bk, bkey)
                    P.act(lambda e, bk=bk, n=n, si=si, half=half: e.copy(
                        out=gvTok[0:n, si, half * 512:(half + 1) * 512], in_=bk[0:n, :]), reads=[bkey, "AR"],
                        writes=["gvTok"])
            s6, ks6 = load_w(w_in[:, C_GLR:C_GLR + 16], 16, 16)
            for kc in range(KC):
                P.pe(lambda e, kc=kc: e.matmul(banks[4][0:16, 0:TW], lhsT=s6[:, kc, 0:16], rhs=hT[:, kc, 0:TW],
                                               start=(kc == 0), stop=(kc == KC - 1)),
                     reads=[ks6, "hT%d" % kc], writes=["bank4"])
            P.act(lambda e: e.copy(out=glrT[:, 0:TW], in_=banks[4][0:16, 0:TW]), reads=["bank4", "AR"], writes=["glrT"])

            CP("m_gproj")
            for si, sub in enumerate(subs):
                n, col0 = sub["n"], sub["col0"]
                sidx = sub["state"]
                S, Sb = Sst[sidx], Sbf[sidx]
                Sk, Sbk = "Sst%d" % sidx, "Sbf%d" % sidx
                P.pe(lambda e: e.matmul(banks[5][0:n, :], lhsT=glrT[:, col0:col0 + n], rhs=wg2b[:], start=True, stop=False),
                     reads=["glrT", "wg2b", "AR"], writes=["bank5"])
                P.pe(lambda e: e.matmul(banks[5][0:n, :], lhsT=ones1[0:1, 0:n], rhs=bgb[:], start=False, stop=True),
                     reads=["ones1", "bgb"], writes=["bank5"])
                P.act(lambda e: e.activation(out=lbuf[0:n, :], in_=banks[5][0:n, :], func=AF.Exp, scale=-1.0),
                      reads=["bank5", "AR"], writes=["lbuf"])
                P.act(lambda e: e.activation(out=lbuf[0:n, :], in_=lbuf[0:n, :], func=AF.Ln, bias=1.0),
                      reads=["lbuf", "AR"], writes=["lbuf"])
                for h in range(4):
                    P.pe(lambda e, h=h: e.matmul(banks[6][:, h * 128:h * 128 + n], lhsT=lbuf[0:n, h * 128:(h + 1) * 128],
                                                 rhs=TRI[0:n, 0:n], start=True, stop=True),
                         reads=["lbuf", "cst", "AR"], writes=["bank6"])
                P.pe(lambda e: e.matmul(banks[5][0:n, :], lhsT=TRIR[0:n, 0:n], rhs=lbuf[0:n, :], start=True, stop=True),
                     reads=["lbuf", "cst", "AR"], writes=["bank5"])
                b6 = banks[6][:].rearrange("p (a b) -> p a b", a=4)
                P.act(lambda e: e.activation(out=eb[:, :, 0:n], in_=b6[:, :, 0:n], func=AF.Exp), reads=["bank6", "AR"],
                      writes=["eb"])
                P.act(lambda e: e.activation(out=enb[:, :, 0:n], in_=b6[:, :, 0:n], func=AF.Exp, scale=-1.0),
                      reads=["bank6", "AR"], writes=["enb"])
                P.dve(lambda e: e.scalar_tensor_tensor(out=qe[:, :, 0:n], in0=gqT[:, :, col0:col0 + n],
                                                       scalar=float(128 ** -0.5), in1=eb[:, :, 0:n], op0=ALU.mult,
                                                       op1=ALU.mult), reads=["gqT", "eb", "AR"], writes=["qe"])
                P.dve(lambda e: e.tensor_tensor(out=ke[:, :, 0:n], in0=gkT[:, :, col0:col0 + n], in1=enb[:, :, 0:n],
                                                op=ALU.mult), reads=["gkT", "enb", "AR"], writes=["ke"])
                P.act(lambda e: e.activation(out=lbuf[0:n, :], in_=banks[5][0:n, :], func=AF.Exp), reads=["bank5", "AR"],
                      writes=["lbuf"])
                P.dve(lambda e: e.tensor_tensor(out=k2[0:n, :], in0=gkTok[0:n, si, :], in1=lbuf[0:n, :], op=ALU.mult),
                      reads=["gkTok", "lbuf", "AR"], writes=["k2"])
                for h in range(4):
                    P.pe(lambda e, h=h: e.matmul(banks[4][0:n, h * 128:h * 128 + n], lhsT=ke[:, h, 0:n], rhs=qe[:, h, 0:n],
                                                 start=True, stop=True), reads=["ke", "qe", "AR"], writes=["bank4"])
                P.dve(lambda e: e.tensor_tensor(out=AmT[0:n, :, 0:n],
                                                in0=banks[4][0:n, :].rearrange("p (a b) -> p a b", a=4)[:, :, 0:n],
                                                in1=CM[0:n, 0:n].unsqueeze(1).broadcast_to([n, 4, n]), op=ALU.mult),
                      reads=["bank4", "cst", "AR"], writes=["AmT"])
                for h in range(4):
                    for ec in range(2):
                        bi_ = (h * 2 + ec) // 4
                        o_ = ((h * 2 + ec) % 4) * 128
                        P.pe(lambda e, h=h, ec=ec, bi_=bi_, o_=o_: e.matmul(
                            banks[bi_][:, o_:o_ + n], lhsT=gvTok[0:n, si, h * 256 + ec * 128:h * 256 + (ec + 1) * 128],
                            rhs=AmT[0:n, h, 0:n], start=True, stop=False), reads=["gvTok", "AmT", "AR"],
                            writes=["bank%d" % bi_])
                        P.pe(lambda e, h=h, ec=ec, bi_=bi_, o_=o_: e.matmul(
                            banks[bi_][:, o_:o_ + n], lhsT=Sb[:, h, ec * 128:(ec + 1) * 128], rhs=qe[:, h, 0:n],
                            start=False, stop=True), reads=[Sbk, "qe", "AR"], writes=["bank%d" % bi_])
                for h in range(4):
                    ub = banks[2 + h % 2]
                    ubk = "bank%d" % (2 + h % 2)
                    P.pe(lambda e, h=h, ub=ub: e.matmul(ub[:, 0:256], lhsT=k2[0:n, h * 128:(h + 1) * 128],
                                                        rhs=gvTok[0:n, si, h * 256:(h + 1) * 256], start=True, stop=True),
                         reads=["k2", "gvTok", "AR"], writes=[ubk])
                    P.dve(lambda e, h=h, ub=ub: e.scalar_tensor_tensor(
                        out=S[:, h, :], in0=S[:, h, :], scalar=eb[:, h, n - 1:n], in1=ub[:, 0:256], op0=ALU.mult,
                        op1=ALU.add), reads=[ubk, "eb", Sk, Sbk, "AR"], writes=[Sk])
                    P.act(lambda e, h=h: e.copy(out=Sb[:, h, :], in_=S[:, h, :]), reads=[Sk], writes=[Sbk])
                for h in range(4):
                    for ec in range(2):
                        bi_ = (h * 2 + ec) // 4
                        o_ = ((h * 2 + ec) % 4) * 128
                        s = sqb[ec]
                        P.act(lambda e, bi_=bi_, o_=o_, s=s: e.activation(out=s[:, 0:n], in_=banks[bi_][:, o_:o_ + n],
                                                                          func=AF.Square),
                              reads=["bank%d" % bi_, "otok"], writes=["sqb0"])
                        P.pe(lambda e, ec=ec, s=s: e.matmul(banks[7][:, 0:n], lhsT=ones256[:], rhs=s[:, 0:n],
                                                            start=(ec == 0), stop=(ec == 1)),
                             reads=["ones256", "sqb0"], writes=["bank7"])
                    P.act(lambda e: e.activation(out=rs2[:, 0:n], in_=banks[7][:, 0:n], func=AF.Ln, bias=epsc[:, 0:1]),
                          reads=["bank7", "epsc", "AR"], writes=["rs2"])
                    P.act(lambda e: e.activation(out=rs2[:, 0:n], in_=rs2[:, 0:n], func=AF.Exp, scale=-0.5),
                          reads=["rs2", "AR"], writes=["rs2"])
                    for ec in range(2):
                        bi_ = (h * 2 + ec) // 4
                        o_ = ((h * 2 + ec) % 4) * 128
                        P.dve(lambda e, h=h, ec=ec, bi_=bi_, o_=o_: e.scalar_tensor_tensor(
                            out=catA[:, h * 2 + ec, col0:col0 + n], in0=banks[bi_][:, o_:o_ + n],
                            scalar=gcols[:, 80 + ec:81 + ec], in1=rs2[:, 0:n], op0=ALU.mult, op1=ALU.mult),
                            reads=["bank%d" % bi_, "gcols", "rs2", "AR"], writes=["catA%d" % (h * 2 + ec), "otok"])
            CP("m_gla")
            for half in range(2):
                s7, ks7 = load_w(w_in[:, C_GR + half * 512:C_GR + (half + 1) * 512], 16, 512)
                for j in range(4):
                    cc = half * 4 + j
                    bk, bkey = banks[j % 4], "bank%d" % (j % 4)
                    for kc in range(KC):
                        P.pe(lambda e, kc=kc, j=j, bk=bk, s7=s7: e.matmul(
                            bk[:, 0:TW], lhsT=s7[:, kc, j * 128:(j + 1) * 128], rhs=hT[:, kc, 0:TW], start=(kc == 0),
                            stop=(kc == KC - 1)), reads=[ks7, "hT%d" % kc], writes=[bkey])
                    t = sgt[j % 2]
                    P.act(lambda e, t=t, bk=bk: e.activation(out=t[:, 0:TW], in_=bk[:, 0:TW], func=AF.Silu),
                          reads=[bkey], writes=["sgt0"])
                    P.dve(lambda e, t=t, cc=cc: e.tensor_tensor(out=catA[:, cc, 0:TW], in0=catA[:, cc, 0:TW],
                                                               in1=t[:, 0:TW], op=ALU.mult),
                          reads=["sgt0", "catA%d" % cc, "AR"], writes=["catA%d" % cc])
            apply_wo(catA, ["catA%d" % j for j in range(8)], 0)

        def mem_kv():
            arena_phase()
            load_x_tile(mem, [(0, 0, 128), (128, 128, 128)])
            CP("mem_load")
            norm_F(256, 4)
            CP("mem_norm")
            arena_phase()
            tproc = AR.f32([128, 512])
            gtmp = AR.f32([128, 512])
            gss_ = AR.f32([128, 16])
            oute = [AR.f32([128, 512]) for _ in range(2)]
            tb = AR.bf([128, 512])
            sk_, kk = load_w(wmkv[:, 0:512], 16, 512)
            sv_, kv = load_w(wmkv[:, 512:1024], 16, 512)
            for s in range(2):
                bk, bkey = banks[s], "bank%d" % s
                proj_T(s * 128, 128, sk_, kk, 512, bk, bkey)
                P.act(lambda e, bk=bk: e.copy(out=tproc[:, :], in_=bk[:, :]), reads=[bkey, "AR"], writes=["tproc"])
                CP("mem_proj")
                oe = oute[0]
                groupnorm_T("tproc", tproc[:, :], 128, 4, 128, gbc[:, 256:384], oe[:, :], tb[:, :], gtmp, gss_,
                            ["oute0", "tb"])
                CP("mem_gn")
                store(mkp[s * 128:(s + 1) * 128, :], oe[:, :], ["oute0", "AR"])
                CP("mem_st")
                transpose_T2F(tb, "tb", 128, 4, lambda j, s=s: mkT[0][:, j, s * 128:(s + 1) * 128], ["mkT0"] * 4, 4 + s)
                CP("mem_tr")
                bk2, bkey2 = banks[2 + s], "bank%d" % (2 + s)
                proj_T(s * 128, 128, sv_, kv, 512, bk2, bkey2)
                oe2 = oute[1]
                P.act(lambda e, bk2=bk2, oe2=oe2: e.copy(out=oe2[:, :], in_=bk2[:, :]), reads=[bkey2, "AR"],
                      writes=["oute1"])
                store(mvp[s * 128:(s + 1) * 128, :], oe2[:, :], ["oute1", "AR"])
                CP("mem_v0a")
                P.dve(lambda e, oe2=oe2, s=s: e.tensor_copy(out=mvaug[0][:, s, :, 0:128],
                                                            in_=oe2[:, :].rearrange("p (a b) -> p a b", a=4)),
                      reads=["oute1", "AR"], writes=["mvaug0"])
                CP("mem_v%d" % s)

        pq = [0]

        def pq_next():
            pq[0] += 1
            return pq[0] % 4

        def sample_prologue():
            arena_phase()
            AR.limit = AR_COLS - 5152
            o = AR.limit
            Sst[1] = arena[:, o:o + 2048].bitcast(F32).rearrange("p (a b) -> p a b", a=4)
            Sbf[1] = arena[:, o + 2048:o + 3072].rearrange("p (a b) -> p a b", a=4)
            mkT[1] = arena[:, o + 3072:o + 4096].rearrange("p (a b) -> p a b", a=4)
            mvaug[1] = arena[:, o + 4096:o + 5152].rearrange("p (a b c) -> p a b c", a=2, b=4)
            P.pool(lambda e: e.memset(arena[:, o + 4096:o + 5152], 1.0), reads=["AR"], writes=["mvaug1"])
            kst = AR.bf([128, 8, 512])
            for sq_ in range(2):
                P.dma("pool", "wq%d" % (pq_next()), lambda e, sq_=sq_: e.dma_start(
                    out=kst, in_=cdk[sq_].rearrange("(b p) c -> p b c", p=128)), reads=["AR"], writes=["kst"])
                for b in range(8):
                    o = sq_ * 1024 + b * 128
                    transpose_T2F(kst[:, b, :], "kst", 128, 4, lambda j, o=o: dkT[:, j, o:o + 128], ["dkT"] * 4,
                                  4 + b % 2)
                for b in range(8):
                    P.dma("pool", "wq%d" % (pq_next()), lambda e, sq_=sq_, b=b: e.dma_start(
                        out=vaug[:, sq_ * 8 + b, :, 0:128],
                        in_=cdv[sq_, b * 128:(b + 1) * 128, :].rearrange("p (h d) -> p h d", h=4)),
                        reads=["vaug_init"], writes=["vaug"])
                P.dma("pool", "wq%d" % (pq_next()), lambda e, sq_=sq_: e.dma_start(
                    out=kst[:, 0:2, :], in_=cmk[sq_].rearrange("(b p) c -> p b c", p=128)), reads=["AR"], writes=["kst"])
                for b in range(2):
                    transpose_T2F(kst[:, b, :], "kst", 128, 4, lambda j, b=b, sq_=sq_: mkT[sq_][:, j, b * 128:(b + 1) * 128],
                                  ["mkT%d" % sq_] * 4, 4 + b % 2)
                for b in range(2):
                    P.dma("pool", "wq%d" % (pq_next()), lambda e, sq_=sq_, b=b: e.dma_start(
                        out=mvaug[sq_][:, b, :, 0:128],
                        in_=cmv[sq_, b * 128:(b + 1) * 128, :].rearrange("p (h d) -> p h d", h=4)),
                        reads=["AR"], writes=["mvaug%d" % sq_])
                load(Sst[sq_][:], sgla[sq_].rearrange("h d e -> d h e"), ["Sst%d" % sq_], reads=["AR"])
                P.act(lambda e, sq_=sq_: e.copy(out=Sbf[sq_][:].rearrange("p a b -> p (a b)"),
                                                in_=Sst[sq_][:].rearrange("p a b -> p (a b)")),
                      reads=["Sst%d" % sq_], writes=["Sbf%d" % sq_])

        def main_schedule():
            for _ in range(pad):
                if padeng == "dve":
                    P.dve(lambda e: e.tensor_copy(out=neglam[:, 3:4], in_=neglam[:, 3:4]))
                else:
                    P.pe(lambda e: e.nop())
            CP("setup")
            if do_mem:
                mem_kv()
            P.pool(lambda e: e.memset(Sst[0][:].rearrange("p a b -> p (a b)"), 0.0), writes=["Sst0"])
            P.pool(lambda e: e.memset(Sbf[0][:].rearrange("p a b -> p (a b)"), 0.0), writes=["Sbf0"])
            for ti in range(n_prompt_tiles):
                wpass["mode"] = "first" if do_sample else "once"
                wpass["nt"] = n_prompt_tiles
                wpass["ti"] = ti
                wpass["k"] = 0
                arena_phase()
                r0 = ti * 512
                load_x_tile(xp, [(r0 + s * 128, s * 128, 128) for s in range(4)])
                CP("t_load")
                ffn(512, w1i, w1o, 0)
                CP("t_ffn1")
                subs = [dict(n=128, col0=s * 128, pos0=r0 + s * 128, seq=0, mem=0, state=0,
                             dk_dst=dkp[r0 + s * 128:r0 + (s + 1) * 128, :], dv_dst=dvp[r0 + s * 128:r0 + (s + 1) * 128, :])
                        for s in range(4)]
                mixer(512, subs, "prompt", ti)
                CP("t_mix")
                ffn(512, w2i, w2o, 2)
                CP("t_ffn2")
                store_y_tile(yp, [(r0 + s * 128, s * 128, 128) for s in range(4)], 512, 3)
                flush_wb()
                assert wpass["k"] == NW_LOADS, wpass["k"]
            store(gsp.rearrange("h d e -> d h e"), Sst[0][:], ["Sst0"])
            if do_sample:
                wpass["mode"] = "later" if n_prompt_tiles > 0 else "once"
                wpass["k"] = 0
                sample_prologue()
                arena_phase()
                load_x_tile(xs, [(0, 0, 64)])
                ffn(64, w1i, w1o, 0)
                subs = [dict(n=32, col0=s * 32, pos0=PAST, seq=s, mem=s, state=s, dk_dst=dks[s * 32:(s + 1) * 32, :],
                             dv_dst=dvs[s * 32:(s + 1) * 32, :]) for s in range(2)]
                mixer(64, subs, "sample", 0)
                ffn(64, w2i, w2o, 2)
                store_y_tile(ys, [(0, 0, 64)], 64, 3)
                for s in range(2):
                    store(gss[s].rearrange("h d e -> d h e"), Sst[s][:], ["Sst%d" % s])

        try:
            main_schedule()
        except _StopBuild:
            pass
        fin = sb("fin", [128, 8])
        P.pe(lambda e: e.matmul(banks[7][0:1, 0:8], lhsT=ones1[0:1, 0:1], rhs=ones1[0:1, 0:8], start=True, stop=True),
             reads=["ones1"], writes=["bank7"])
        P.act(lambda e: e.copy(out=fin[:, 0:1], in_=epsc[:, 0:1]), reads=["epsc", "bank7"], writes=["fin_act"])
        P.dve(lambda e: e.tensor_copy(out=fin[:, 1:2], in_=epsc[:, 0:1]), reads=["epsc"], writes=["fin_dve"])
        P.pool(lambda e: e.memset(fin[:, 2:3], 0.0), writes=["fin_pool"])
        P.add("sp", lambda e: e.nop(), reads=list(out_keys) + ["fin_act", "fin_dve", "fin_pool"])
        stats = P.emit(nc, st)
    if DO_COMPILE:
        nc.compile()
    return nc, stats


_CACHE = {}


def kernel(x_prompt, x_sample, mem_prompt, cache_diff_k, cache_diff_v, state_gla, cache_mem_k, cache_mem_v,
           rel_bias_table, norm_ffn1, w_ffn1_in, w_ffn1_out, norm_mix, w_in, w_gla_g2, b_gla_g, gla_out_norm,
           diff_q_norm, diff_k_norm, diff_lambda, diff_out_norm, mem_norm, w_mem_kv, mem_q_norm, mem_k_norm,
           w_o, norm_ffn2, w_ffn2_in, w_ffn2_out, norm_final, _cfg=None):
    f = lambda a: np.ascontiguousarray(np.asarray(a, dtype=np.float32))
    cfg = _cfg or dict(n_prompt_tiles=4, do_sample=True, do_mem=True)
    key = tuple(sorted(cfg.items()))
    if key not in _CACHE:
        _CACHE[key] = build_program(**{k: v for k, v in cfg.items() if k != "cores"})
        print("stats", _CACHE[key][1])
    nc, stats = _CACHE[key]
    nvec = np.stack([f(norm_ffn1)[0], f(norm_mix)[0], f(norm_ffn2)[0], f(norm_final)[0], f(mem_norm)[0]])
    shared = dict(
        table=f(rel_bias_table).reshape(128), nvec=f(nvec), glan=f(gla_out_norm)[0],
        w1i=f(w_ffn1_in)[0], w1o=f(w_ffn1_out)[0], w_in=f(w_in)[0], wg2=f(w_gla_g2)[0], bg=f(b_gla_g)[0][None, :],
        dqn=f(diff_q_norm)[0], dkn=f(diff_k_norm)[0], dlam=f(diff_lambda)[0].reshape(256), don=f(diff_out_norm)[0],
        wmkv=f(w_mem_kv)[0], mqn=f(mem_q_norm)[0], mkn=f(mem_k_norm)[0], wo=f(w_o)[0], w2i=f(w_ffn2_in)[0],
        w2o=f(w_ffn2_out)[0], cst=_consts())
    xp_, xs_, mem_ = f(x_prompt), f(x_sample), f(mem_prompt)
    cdk_, cdv_, sg_ = f(cache_diff_k)[0], f(cache_diff_v)[0], f(state_gla)[0]
    cmk_, cmv_ = f(cache_mem_k)[0], f(cache_mem_v)[0]
    in_maps = []
    for c in range(8):
        m = dict(shared)
        m.update(xp=xp_[c], xs=xs_[2 * c:2 * c + 2].reshape(64, D), mem=mem_[c],
                 cdk=cdk_[2 * c:2 * c + 2].reshape(2, PAST, 512), cdv=cdv_[2 * c:2 * c + 2].reshape(2, PAST, 512),
                 sgla=sg_[2 * c:2 * c + 2], cmk=cmk_[2 * c:2 * c + 2].reshape(2, NMEM, 512),
                 cmv=cmv_[2 * c:2 * c + 2].reshape(2, NMEM, 512))
        in_maps.append(m)
    ncores = cfg.get("cores", 8) if _cfg else 8
    if cfg.get("tiny"):
        for m in in_maps:
            for k in ("w1i", "w1o", "w_in", "wmkv", "wo", "w2i", "w2o", "xp"):
                m[k] = np.ascontiguousarray(m[k][:128, :128])
    res = run_bass_kernel_spmd(nc, in_maps[:ncores], core_ids=list(range(ncores)))
    R = list(res.results) + [res.results[0]] * (8 - ncores)
    g = lambda name: np.stack([np.asarray(R[c][name], dtype=np.float32) for c in range(8)])
    y_p = g("yp")
    y_s = g("ys").reshape(16, 32, D)
    dk_p = g("dkp").reshape(1, 8, SEQ, 4, 128)
    dv_p = g("dvp").reshape(1, 8, SEQ, 4, 128)
    gs_p = g("gsp").reshape(1, 8, 4, 128, 256)
    mk_p = g("mkp").reshape(1, 8, NMEM, 4, 128)
    mv_p = g("mvp").reshape(1, 8, NMEM, 4, 128)
    dk_s = g("dks").reshape(1, 16, 32, 4, 128)
    dv_s = g("dvs").reshape(1, 16, 32, 4, 128)
    gs_s = g("gss").reshape(1, 16, 4, 128, 256)
    return (y_p, y_s, dk_p, dv_p, gs_p, mk_p, mv_p, dk_s, dv_s, gs_s)
```

```python
import contextlib
import types
import numpy as np
import concourse.bass as bass
import concourse.mybir as mybir
from concourse.bass_utils import run_bass_kernel_spmd

F32 = mybir.dt.float32
BF16 = mybir.dt.bfloat16
AF = mybir.ActivationFunctionType
ALU = mybir.AluOpType
AX = mybir.AxisListType

EPOCH = 6000
DO_COMPILE = False
DMA_EPOCH = 300

D = 2048
KC = 16
FF = 5504
NFF = 43
SEQ = 2048
PAST = 1024
NMEM = 256
INW = 5136
EPS = 1e-6
LAM_INIT = 0.2
C_GQ, C_GK, C_GV, C_GR, C_GLR, C_DQ, C_DK, C_DV, C_MQ = 0, 512, 1024, 2048, 3072, 3088, 3600, 4112, 4624


class _StopBuild(Exception):
    pass


class Op:
    __slots__ = ("eng", "fn", "reads", "writes", "dma", "idx", "waits", "signal", "sem", "val")

    def __init__(self, eng, fn, reads, writes, dma):
        self.eng = eng
        self.fn = fn
        self.reads = tuple(reads)
        self.writes = tuple(writes)
        self.dma = dma
        self.waits = []
        self.signal = False
        self.sem = None
        self.val = 0


def _freeze(fn, depth=0):
    if not isinstance(fn, types.FunctionType) or fn.__closure__ is None or depth > 3:
        return fn
    cells = []
    for c in fn.__closure__:
        try:
            v = c.cell_contents
        except ValueError:
            cells.append(c)
            continue
        if isinstance(v, types.FunctionType) and v.__code__.co_filename == fn.__code__.co_filename:
            v = _freeze(v, depth + 1)
        cells.append(types.CellType(v))
    g = types.FunctionType(fn.__code__, fn.__globals__, fn.__name__, fn.__defaults__, tuple(cells))
    g.__kwdefaults__ = fn.__kwdefaults__
    return g


class Prog:
    ENGS = ("pe", "act", "dve", "pool", "sp")

    def __init__(self):
        self.ops = []

    def add(self, eng, fn, reads=(), writes=(), dma=None):
        fn = _freeze(fn)
        op = Op(eng, fn, reads, writes, dma)
        op.idx = len(self.ops)
        self.ops.append(op)
        return op

    def pe(self, fn, reads=(), writes=()):
        return self.add("pe", fn, reads, writes)

    def act(self, fn, reads=(), writes=()):
        return self.add("act", fn, reads, writes)

    def dve(self, fn, reads=(), writes=()):
        return self.add("dve", fn, reads, writes)

    def pool(self, fn, reads=(), writes=()):
        return self.add("pool", fn, reads, writes)

    def dma(self, queue, sem, fn, reads=(), writes=()):
        return self.add(queue, fn, reads, writes, dma=sem)

    def analyze(self):
        last_w = {}
        rd_eng = {}
        rd_dma = {}
        dep_lists = []
        for op in self.ops:
            deps = {}

            def add(d, kind):
                if d is op:
                    return
                if d.dma is None and op.dma is None and d.eng == op.eng:
                    if op.eng == "pe":
                        return
                deps[d.idx] = d

            for k in op.reads:
                w = last_w.get(k)
                if w is not None:
                    add(w, "RAW")
            for k in op.writes:
                w = last_w.get(k)
                if w is not None:
                    add(w, "WAW")
                for r in rd_eng.get(k, {}).values():
                    add(r, "WAR")
                for r in rd_dma.get(k, ()):
                    add(r, "WAR")
            for k in op.reads:
                if op.dma is not None:
                    rd_dma.setdefault(k, []).append(op)
                else:
                    rd_eng.setdefault(k, {})[op.eng] = op
            for k in op.writes:
                last_w[k] = op
                rd_eng[k] = {}
                rd_dma[k] = []
            dep_lists.append(list(deps.values()))
        for op, deps in zip(self.ops, dep_lists):
            for d in deps:
                d.signal = True
        cnt = {}
        self.sem_names = []
        for op in self.ops:
            if op.dma is not None:
                key = ("dma", op.dma)
                per = DMA_EPOCH
                step = 16
            elif op.signal:
                key = ("eng", op.eng)
                per = EPOCH
                step = 1
            else:
                continue
            n = cnt.get(key, 0)
            cnt[key] = n + 1
            name = "%s_%s_%d" % (key[0], key[1], n // per)
            if n % per == 0:
                self.sem_names.append(name)
            op.sem = name
            op.val = (n % per + 1) * step
        for op, deps in zip(self.ops, dep_lists):
            w = {}
            for d in deps:
                if w.get(d.sem, 0) < d.val:
                    w[d.sem] = d.val
            if op.dma is not None and op.val > 16:
                if w.get(op.sem, 0) < op.val - 16:
                    w[op.sem] = op.val - 16
            op.waits = sorted(w.items())
        return self

    def emit(self, nc, stack):
        self.analyze()
        sems = {}
        for name in self.sem_names:
            sems[name] = stack.enter_context(nc.semaphore(name))
        by_eng = {e: [op for op in self.ops if op.eng == e] for e in self.ENGS}
        block = stack.enter_context(nc.Block())
        stats = {e: [0, 0] for e in self.ENGS}
        final_vals = {}
        for op in self.ops:
            if op.sem is not None:
                final_vals[op.sem] = max(final_vals.get(op.sem, 0), op.val)

        def run(e, eng):
            seen = {}
            for op in by_eng[e]:
                for (s, v) in op.waits:
                    if seen.get(s, 0) < v:
                        eng.wait_ge(sems[s], v)
                        seen[s] = v
                        stats[e][1] += 1
                inst = op.fn(eng)
                stats[e][0] += 1
                if op.dma is not None:
                    inst.then_inc(sems[op.sem], 16)
                elif op.signal:
                    inst.then_inc(sems[op.sem], 1)

        @block.tensor
        def _(eng):
            run("pe", eng)

        @block.scalar
        def _(eng):
            run("act", eng)

        @block.vector
        def _(eng):
            run("dve", eng)

        @block.gpsimd
        def _(eng):
            run("pool", eng)

        @block.sync
        def _(eng):
            run("sp", eng)
            for name, v in sorted(final_vals.items()):
                if name.startswith("dma_"):
                    eng.wait_ge(sems[name], v)

        self.stats = stats
        return stats


def _bucket(rel):
    nb, max_exact = 16, 8
    ret = np.where(rel > 0, nb, 0)
    n = np.abs(rel)
    nf = np.maximum(n, 1).astype(np.float32)
    large = max_exact + (np.log(nf / max_exact) / np.float32(np.log(128 / max_exact)) * (nb - max_exact)).astype(np.int32)
    large = np.minimum(large, nb - 1)
    return ret + np.where(n < max_exact, n, large)


def _consts():
    k = np.arange(128)[:, None]
    q = np.arange(128)[None, :]
    c = np.zeros((128, 6, 128), np.float32)
    c[:, 0] = np.eye(128)
    c[:, 1] = (k <= q)
    c[:, 2] = (k <= q) * (-1.0 / 16.0)
    c[:, 3] = (k > q) * (-1.0 / 16.0)
    c[:, 4] = _bucket(k - q)
    c[:, 5] = _bucket(k - q - 128)
    return c.reshape(128, 6 * 128)


def build_program(n_prompt_tiles=4, do_sample=True, do_mem=True, stop=None, tiny=False, pad=0, padeng="dve"):
    nc = bass.Bass("TRN2", target_bir_lowering=False)
    P = Prog()

    def din(name, shape):
        if tiny and name in ("w1i", "w1o", "w_in", "wmkv", "wo", "w2i", "w2o", "xp"):
            shape = [128, 128]
        return nc.dram_tensor(name, list(shape), F32, kind="ExternalInput").ap()

    def dout(name, shape):
        return nc.dram_tensor(name, list(shape), F32, kind="ExternalOutput").ap()

    xp = din("xp", [SEQ, D]); xs = din("xs", [64, D]); mem = din("mem", [NMEM, D])
    cdk = din("cdk", [2, PAST, 512]); cdv = din("cdv", [2, PAST, 512]); sgla = din("sgla", [2, 4, 128, 256])
    cmk = din("cmk", [2, NMEM, 512]); cmv = din("cmv", [2, NMEM, 512])
    table = din("table", [128]); nvec = din("nvec", [5, D]); glan = din("glan", [256])
    w1i = din("w1i", [D, 2 * FF]); w1o = din("w1o", [FF, D]); w_in = din("w_in", [D, INW])
    wg2 = din("wg2", [16, 512]); bg = din("bg", [1, 512])
    dqn = din("dqn", [64]); dkn = din("dkn", [64]); dlam = din("dlam", [256]); don = din("don", [128])
    wmkv = din("wmkv", [D, 1024]); mqn = din("mqn", [128]); mkn = din("mkn", [128])
    wo = din("wo", [D, D]); w2i = din("w2i", [D, 2 * FF]); w2o = din("w2o", [FF, D])
    cst = din("cst", [128, 768])
    yp = dout("yp", [SEQ, D]); ys = dout("ys", [64, D])
    dkp = dout("dkp", [SEQ, 512]); dvp = dout("dvp", [SEQ, 512]); gsp = dout("gsp", [4, 128, 256])
    mkp = dout("mkp", [NMEM, 512]); mvp = dout("mvp", [NMEM, 512])
    dks = dout("dks", [64, 512]); dvs = dout("dvs", [64, 512]); gss = dout("gss", [2, 4, 128, 256])

    NW_LOADS = 2 * (22 + 12) + 11 + 8
    wscr = nc.dram_tensor("wscr", [128, NW_LOADS * 16 * 512], BF16).ap()
    st = contextlib.ExitStack()
    with st:
        def sb(name, shape, dt=F32):
            return st.enter_context(nc.sbuf_tensor(name, shape, dt))

        xT = sb("xT", [128, KC, 512])
        hT = sb("hT", [128, KC, 512], BF16)
        NSLOT = 3
        slots = [sb("slot%d" % i, [128, 16, 512], BF16) for i in range(NSLOT)]
        AR_COLS = 22016
        arena = sb("arena", [128, AR_COLS], BF16)
        dkT = sb("dkT", [128, 4, SEQ], BF16)
        vaug = sb("vaug", [128, 16, 4, 132], BF16)
        dkTn = sb("dkTn", [128, 4, 64], BF16)
        vnew = sb("vnew", [32, 2, 4, 132], BF16)
        mkT = [sb("mkT0", [128, 4, NMEM], BF16), None]
        mvaug = [sb("mvaug0", [128, 2, 4, 132], BF16), None]
        Sst = [sb("Sst0", [128, 4, 256]), None]
        Sbf = [sb("Sbf0", [128, 4, 256], BF16), None]
        cst_sb = sb("cst_sb", [128, 6, 128])
        identb = sb("identb", [128, 128], BF16)
        onesD = sb("onesD", [128, 128], BF16)
        ones256 = sb("ones256", [128, 128], BF16)
        ones1 = sb("ones1", [1, 128], BF16)
        gcols = sb("gcols", [128, 88])
        tab_bc = sb("tab_bc", [128, 128])
        gbc = sb("gbc", [128, 512])
        neglam = sb("neglam", [128, 4])
        wg2b = sb("wg2b", [16, 512], BF16)
        bgb = sb("bgb", [1, 512], BF16)
        BIAS = sb("BIAS", [128, 4, 2, 128])
        sqb0 = sb("sqb0", [128, 512], BF16)
        sqb1 = sb("sqb1", [128, 512], BF16)
        sqb = [sqb0, sqb1]
        rstd = sb("rstd", [128, 512])
        sgt0 = sb("sgt0", [128, 512])
        sgt = [sgt0, sgt0]
        banks = [st.enter_context(nc.psum_tensor("bank%d" % i, [128, 512], F32)) for i in range(8)]
        epsc = sb("epsc", [128, 1])
        bar_scr = sb("bar_scr", [128, 2])
        P.pool(lambda e: e.memset(epsc[:], EPS), writes=["epsc"])

        ident = cst_sb[:, 0, :]
        CM = cst_sb[:, 1, :]
        TRI = cst_sb[:, 2, :]
        TRIR = cst_sb[:, 3, :]

        class Arena:
            def __init__(self):
                self.off = 0
                self.limit = AR_COLS

            def reset(self):
                self.off = 0

            def take(self, cols_bf16):
                o = self.off
                self.off += cols_bf16
                assert self.off <= self.limit, ("arena overflow", self.off, self.limit)
                return arena[:, o:o + cols_bf16]

            def bf(self, shape):
                n = int(np.prod(shape[1:]))
                v = self.take(n)
                return v if len(shape) == 2 else v.rearrange(_pat(len(shape)), **_dims(shape))

            def f32(self, shape):
                n = int(np.prod(shape[1:]))
                v = self.take(2 * n).bitcast(F32)
                return v if len(shape) == 2 else v.rearrange(_pat(len(shape)), **_dims(shape))

        def _pat(nd):
            names = "abcd"[:nd - 1]
            return "p (%s) -> p %s" % (" ".join(names), " ".join(names))

        def _dims(shape):
            names = "abcd"[:len(shape) - 1]
            return {n: s for n, s in zip(names[:-1], shape[1:-1])}

        AR = Arena()
        phase = [0]

        def arena_phase():
            P.act(lambda e: e.copy(out=bar_scr[:, 0:1], in_=epsc[:, 0:1]), reads=["epsc"], writes=["AR"])
            AR.reset()

        def CP(name):
            if stop == name:
                raise _StopBuild()

        uid = [0]

        def U(prefix):
            uid[0] += 1
            return "%s#%d" % (prefix, uid[0])

        slot_i = [0]
        wpass = {"mode": "once", "k": 0}
        SCR_COLS = 16 * 512
        scr_regions = {}
        pending_wb = []

        def scr_ap(k):
            return wscr[:, k * SCR_COLS:(k + 1) * SCR_COLS]

        def flush_wb(keep=0):
            while len(pending_wb) > keep:
                pending_wb.pop(0)()

        def after_load(i, sl, keys):
            if wpass["mode"] != "first":
                return
            k = wpass["k"]
            if k % wpass.get("nt", 1) != wpass.get("ti", 0):
                flush_wb()
                return
            pending_wb.append(lambda: P.dma(
                "pool", "wb%d" % (k % 2),
                lambda e: e.dma_start(out=scr_ap(k), in_=sl[:].rearrange("p a b -> p (a b)")),
                reads=list(keys), writes=["scr%d" % k]))
            flush_wb(keep=1)

        def load_from_scr(i, sl, keys):
            k = wpass["k"]
            P.dma("pool", "w%d" % i,
                  lambda e: e.dma_start(out=sl[:].rearrange("p a b -> p (a b)"), in_=scr_ap(k)),
                  reads=["scr%d" % k], writes=list(keys))

        def load_w(src_ap, nk, ncols):
            i = slot_i[0] % NSLOT
            slot_i[0] += 1
            s = slots[i]
            key = "slot%d" % i
            if wpass["mode"] == "later":
                load_from_scr(i, s, [key, key + "u"])
            else:
                P.dma("pool", "w%d" % i,
                      lambda e: e.dma_start(out=s[:, 0:nk, 0:ncols], in_=src_ap.rearrange("(k p) c -> p k c", p=128)),
                      writes=[key, key + "u"])
                after_load(i, s, [key, key + "u"])
            wpass["k"] += 1
            return s, key

        out_keys = []
        oq = [0]

        def store(dst_ap, src_ap, reads):
            k = U("out")
            out_keys.append(k)
            oq[0] += 1
            P.dma("sp", "st%d" % (oq[0] % 4), lambda e: e.dma_start(out=dst_ap, in_=src_ap), reads=reads, writes=[k])

        lq = [0]

        def load(dst_ap, src_ap, writes, reads=(), queue="sp", **kw):
            lq[0] += 1
            P.dma(queue, "ld%d" % (lq[0] % 4), lambda e: e.dma_start(out=dst_ap, in_=src_ap, **kw), reads=reads,
                  writes=writes)

        if stop == "pre":
            P.add("sp", lambda e: e.nop(), reads=[])
            stats = P.emit(nc, st)
            return nc, stats
        load(cst_sb[:].rearrange("p a b -> p (a b)"), cst, ["cst"])
        load(tab_bc[:], table.partition_broadcast(128), ["tab"])
        gb_src = [(dqn, 0, 64), (dkn, 64, 64), (mqn, 128, 128), (mkn, 256, 128), (don, 384, 128)]
        for (src, o, n) in gb_src:
            load(gbc[:, o:o + n], src.partition_broadcast(128), ["gbc"])
        AR.reset()
        grow = AR.f32([128, 128])
        dl = AR.f32([128, 256])
        dlp = AR.f32([128, 128])
        P.pool(lambda e: e.memset(grow, 0.0), reads=["AR"], writes=["grow"])
        load(grow[0:80, :], nvec.rearrange("v (c p) -> (v c) p", p=128), ["grow"], reads=["AR"])
        load(grow[80:82, :], glan.rearrange("(c p) -> c p", p=128), ["grow"], reads=["AR"])
        load(dl, dlam.partition_broadcast(128), ["dl"], reads=["AR"])
        P.dma("pool", "wq", lambda e: e.dma_start(out=wg2b[:], in_=wg2), writes=["wg2b"])
        P.dma("pool", "wq", lambda e: e.dma_start(out=bgb[:], in_=bg), writes=["bgb"])
        P.pool(lambda e: e.memset(onesD[:], 1.0 / D), writes=["onesD"])
        P.pool(lambda e: e.memset(ones256[:], 1.0 / 256), writes=["ones256"])
        P.pool(lambda e: e.memset(ones1[:], 1.0), writes=["ones1"])
        P.dve(lambda e: e.memset(vaug[:].rearrange("p a b c -> p (a b c)"), 1.0), writes=["vaug_init"])
        P.dve(lambda e: e.memset(vnew[:].rearrange("p a b c -> p (a b c)"), 1.0), writes=["vnew_init"])
        P.dve(lambda e: e.memset(mvaug[0][:].rearrange("p a b c -> p (a b c)"), 1.0), writes=["mvaug0"])
        P.dve(lambda e: e.tensor_copy(out=identb[:], in_=ident), reads=["cst"], writes=["identb"])
        P.pe(lambda e: e.transpose(out=banks[0][:, 0:128], in_=grow, identity=ident), reads=["grow", "cst", "AR"],
             writes=["bank0"])
        P.dve(lambda e: e.tensor_copy(out=gcols[:], in_=banks[0][:, 0:88]), reads=["bank0"], writes=["gcols"])
        P.dve(lambda e: e.tensor_tensor(out=dlp[:, 0:64], in0=dl[:, 0:64], in1=dl[:, 64:128], op=ALU.mult),
              reads=["dl", "AR"], writes=["dlp"])
        P.dve(lambda e: e.tensor_tensor(out=dlp[:, 64:128], in0=dl[:, 128:192], in1=dl[:, 192:256], op=ALU.mult),
              reads=["dl", "AR"], writes=["dlp"])
        P.dve(lambda e: e.tensor_reduce(out=neglam[:, 0:2], in_=dlp.rearrange("p (a b) -> p a b", a=2), axis=AX.X,
                                        op=ALU.add), reads=["dlp", "AR"], writes=["nl0"])
        P.act(lambda e: e.activation(out=neglam[:, 2:4], in_=neglam[:, 0:2], func=AF.Exp), reads=["nl0"], writes=["nl1"])
        P.dve(lambda e: e.tensor_tensor(out=neglam[:, 0:1], in0=neglam[:, 3:4], in1=neglam[:, 2:3], op=ALU.subtract),
              reads=["nl1"], writes=["nl2"])
        P.dve(lambda e: e.tensor_scalar(out=neglam[:, 0:1], in0=neglam[:, 0:1], scalar1=-LAM_INIT, scalar2=None,
                                        op0=ALU.add), reads=["nl2"], writes=["neglam"])
        P.dve(lambda e: e.tensor_scalar(out=gbc[:, 384:512], in0=gbc[:, 384:512], scalar1=1.0 - LAM_INIT, scalar2=None,
                                        op0=ALU.mult), reads=["gbc"], writes=["gbc"])
        deferred = []
        btmp = rstd[:, 0:128]

        def _bias_ops():
            for ty in range(2):
                BK = cst_sb[:, 4 + ty, :]
                brange = range(0, 32) if ty == 0 else range(1, 16)
                for h in range(4):
                    dst = BIAS[:, h, ty, :]
                    first = True
                    for b in brange:
                        col = b * 4 + h
                        if first:
                            deferred.append(lambda dst=dst, BK=BK, b=b, col=col: P.dve(
                                lambda e: e.tensor_scalar(out=dst, in0=BK, scalar1=float(b),
                                                          scalar2=tab_bc[:, col:col + 1], op0=ALU.is_equal,
                                                          op1=ALU.mult), reads=["cst", "tab"], writes=["BIAS"]))
                            first = False
                        else:
                            deferred.append(lambda BK=BK, b=b, col=col: P.dve(
                                lambda e: e.tensor_scalar(out=btmp, in0=BK, scalar1=float(b),
                                                          scalar2=tab_bc[:, col:col + 1], op0=ALU.is_equal,
                                                          op1=ALU.mult), reads=["cst", "tab"], writes=["rstd"]))
                            deferred.append(lambda dst=dst: P.dve(
                                lambda e: e.tensor_tensor(out=dst, in0=dst, in1=btmp, op=ALU.add),
                                reads=["rstd", "BIAS"], writes=["BIAS"]))
            for h in range(4):
                deferred.append(lambda h=h: P.dve(
                    lambda e: e.tensor_scalar(out=BIAS[:, h, :, :], in0=BIAS[:, h, :, :],
                                              scalar1=tab_bc[:, 60 + h:61 + h], scalar2=None, op0=ALU.subtract),
                    reads=["BIAS", "tab"], writes=["BIAS"]))
            deferred.append(lambda: P.dve(lambda e: e.memset(BIAS[64:128, :, 0, 0:64], -30000.0), reads=["BIAS"],
                                          writes=["BIAS"]))

        _bias_ops()

        def sprinkle(k):
            for _ in range(min(k, len(deferred))):
                deferred.pop(0)()

        def norm_F(TW, gi):
            bk = banks[7]
            for c in range(KC):
                s = sqb[c % 2]
                P.act(lambda e, c=c, s=s: e.activation(out=s[:, 0:TW], in_=xT[:, c, 0:TW], func=AF.Square),
                      reads=["xT%d" % c], writes=["sqb%d" % (c % 2)])
                P.pe(lambda e, c=c, s=s: e.matmul(bk[:, 0:TW], lhsT=onesD[:], rhs=s[:, 0:TW], start=(c == 0),
                                                  stop=(c == KC - 1)),
                     reads=["onesD", "sqb%d" % (c % 2)], writes=["bank7"])
            P.act(lambda e: e.activation(out=rstd[:, 0:TW], in_=bk[:, 0:TW], func=AF.Ln, bias=epsc[:, 0:1]),
                  reads=["bank7", "epsc"], writes=["rstd"])
            P.act(lambda e: e.activation(out=rstd[:, 0:TW], in_=rstd[:, 0:TW], func=AF.Exp, scale=-0.5),
                  reads=["rstd"], writes=["rstd"])
            for c in range(KC):
                P.dve(lambda e, c=c: e.scalar_tensor_tensor(out=hT[:, c, 0:TW], in0=xT[:, c, 0:TW],
                                                            scalar=gcols[:, gi * 16 + c:gi * 16 + c + 1],
                                                            in1=rstd[:, 0:TW], op0=ALU.mult, op1=ALU.mult),
                      reads=["xT%d" % c, "gcols", "rstd"], writes=["hT%d" % c])


        def load_x_tile(src, subs):
            AR.reset()
            stg = [AR.f32([128, D]) for _ in range(2)]
            for si, (row0, col0, n) in enumerate(subs):
                sg = stg[si % 2]
                sk = "stg%d" % (si % 2)
                load(sg[0:n, :], src[row0:row0 + n, :], [sk], reads=["AR"])
                for q in range(4):
                    bk = banks[q % 4]
                    for j in range(4):
                        c = q * 4 + j
                        P.pe(lambda e, sg=sg, c=c, j=j, bk=bk, n=n: e.transpose(
                            out=bk[:, j * 128:j * 128 + n], in_=sg[0:n, c * 128:(c + 1) * 128], identity=ident[0:n, 0:n]),
                            reads=[sk, "cst", "AR"], writes=["bank%d" % (q % 4)])
                    eng = P.act if q % 2 == 0 else P.dve
                    if q % 2 == 0:
                        P.act(lambda e, q=q, bk=bk, n=n, col0=col0: e.copy(
                            out=xT[:, q * 4:q * 4 + 4, col0:col0 + n],
                            in_=bk[:].rearrange("p (a b) -> p a b", a=4)[:, :, 0:n]),
                            reads=["bank%d" % (q % 4)], writes=["xT%d" % (q * 4 + j) for j in range(4)])
                    else:
                        P.dve(lambda e, q=q, bk=bk, n=n, col0=col0: e.tensor_copy(
                            out=xT[:, q * 4:q * 4 + 4, col0:col0 + n],
                            in_=bk[:].rearrange("p (a b) -> p a b", a=4)[:, :, 0:n]),
                            reads=["bank%d" % (q % 4)], writes=["xT%d" % (q * 4 + j) for j in range(4)])

        def ffn(TW, wi, wo_, gi):
            norm_F(TW, gi)
            arena_phase()
            actT = AR.bf([128, NFF, TW])
            bi = 0
            for c0 in range(0, FF, 256):
                ncol = min(256, FF - c0)
                i = slot_i[0] % NSLOT
                slot_i[0] += 1
                sg_ = slots[i]
                kg = "slot%d" % i
                if wpass["mode"] == "later":
                    load_from_scr(i, sg_, [kg, kg + "u"])
                else:
                    P.dma("pool", "w%d" % i, lambda e, sg_=sg_, c0=c0, ncol=ncol: e.dma_start(
                        out=sg_[:, :, 0:ncol], in_=wi[:, c0:c0 + ncol].rearrange("(k p) c -> p k c", p=128)), writes=[kg])
                    P.dma("pool", "wu%d" % i, lambda e, sg_=sg_, c0=c0, ncol=ncol: e.dma_start(
                        out=sg_[:, :, 256:256 + ncol],
                        in_=wi[:, FF + c0:FF + c0 + ncol].rearrange("(k p) c -> p k c", p=128)), writes=[kg + "u"])
                    after_load(i, sg_, [kg, kg + "u"])
                wpass["k"] += 1
                for j in range(ncol // 128):
                    f = c0 // 128 + j
                    bG = banks[(bi % 2) * 2]
                    bU = banks[(bi % 2) * 2 + 1]
                    kG = "bank%d" % ((bi % 2) * 2)
                    kU = "bank%d" % ((bi % 2) * 2 + 1)
                    for kc in range(KC):
                        P.pe(lambda e, kc=kc, j=j, bG=bG, sg_=sg_: e.matmul(
                            bG[:, 0:TW], lhsT=sg_[:, kc, j * 128:(j + 1) * 128], rhs=hT[:, kc, 0:TW], start=(kc == 0),
                            stop=(kc == KC - 1)), reads=[kg, "hT%d" % kc], writes=[kG])
                    for kc in range(KC):
                        P.pe(lambda e, kc=kc, j=j, bU=bU, sg_=sg_: e.matmul(
                            bU[:, 0:TW], lhsT=sg_[:, kc, 256 + j * 128:256 + (j + 1) * 128], rhs=hT[:, kc, 0:TW],
                            start=(kc == 0), stop=(kc == KC - 1)), reads=[kg + "u", "hT%d" % kc], writes=[kU])
                    t = sgt[bi % 2]
                    P.act(lambda e, t=t, bG=bG: e.activation(out=t[:, 0:TW], in_=bG[:, 0:TW], func=AF.Silu),
                          reads=[kG], writes=["sgt0"])
                    P.dve(lambda e, t=t, bU=bU, f=f: e.tensor_tensor(out=actT[:, f, 0:TW], in0=t[:, 0:TW],
                                                                    in1=bU[:, 0:TW], op=ALU.mult),
                          reads=["sgt0", kU, "AR"], writes=["actT%d" % f])
                    bi += 1
                    sprinkle(10)
            CP("ffn_in")
            for slab in range(4):
                for (f0, nf) in ((0, 16), (16, 16), (32, 11)):
                    s_, ks = load_w(wo_[f0 * 128:(f0 + nf) * 128, slab * 512:(slab + 1) * 512], nf, 512)
                    for ff in range(f0, f0 + nf):
                        for dc in range(4):
                            P.pe(lambda e, ff=ff, dc=dc, s_=s_, f0=f0: e.matmul(
                                banks[4 + dc][:, 0:TW], lhsT=s_[:, ff - f0, dc * 128:(dc + 1) * 128],
                                rhs=actT[:, ff, 0:TW], start=(ff == 0), stop=(ff == NFF - 1)),
                                reads=[ks, "actT%d" % ff, "AR"], writes=["bank%d" % (4 + dc)])
                for dc in range(4):
                    c = slab * 4 + dc
                    P.dve(lambda e, c=c, dc=dc: e.scalar_tensor_tensor(
                        out=xT[:, c, 0:TW], in0=banks[4 + dc][:, 0:TW], scalar=0.5, in1=xT[:, c, 0:TW], op0=ALU.mult,
                        op1=ALU.add), reads=["bank%d" % (4 + dc), "xT%d" % c], writes=["xT%d" % c])

        def store_y_tile(dst, subs, TW, gi):
            norm_F_out(TW, gi, dst, subs)

        def norm_F_out(TW, gi, dst, subs):
            bk = banks[7]
            for c in range(KC):
                s = sqb[c % 2]
                P.act(lambda e, c=c, s=s: e.activation(out=s[:, 0:TW], in_=xT[:, c, 0:TW], func=AF.Square),
                      reads=["xT%d" % c], writes=["sqb%d" % (c % 2)])
                P.pe(lambda e, c=c, s=s: e.matmul(bk[:, 0:TW], lhsT=onesD[:], rhs=s[:, 0:TW], start=(c == 0),
                                                  stop=(c == KC - 1)),
                     reads=["onesD", "sqb%d" % (c % 2)], writes=["bank7"])
            P.act(lambda e: e.activation(out=rstd[:, 0:TW], in_=bk[:, 0:TW], func=AF.Ln, bias=epsc[:, 0:1]),
                  reads=["bank7", "epsc"], writes=["rstd"])
            P.act(lambda e: e.activation(out=rstd[:, 0:TW], in_=rstd[:, 0:TW], func=AF.Exp, scale=-0.5),
                  reads=["rstd"], writes=["rstd"])
            for c in range(KC):
                P.dve(lambda e, c=c: e.scalar_tensor_tensor(out=xT[:, c, 0:TW], in0=xT[:, c, 0:TW],
                                                            scalar=gcols[:, gi * 16 + c:gi * 16 + c + 1],
                                                            in1=rstd[:, 0:TW], op0=ALU.mult, op1=ALU.mult),
                      reads=["xT%d" % c, "gcols", "rstd"], writes=["xT%d" % c])
            arena_phase()
            stg = [AR.f32([128, D]) for _ in range(2)]
            for si, (row0, col0, n) in enumerate(subs):
                sg = stg[si % 2]
                sk = "ostg%d" % (si % 2)
                for q in range(4):
                    bk2 = banks[q % 4]
                    for j in range(4):
                        c = q * 4 + j
                        P.pe(lambda e, c=c, j=j, bk2=bk2, n=n, col0=col0: e.transpose(
                            out=bk2[0:n, j * 128:(j + 1) * 128], in_=xT[:, c, col0:col0 + n], identity=ident),
                            reads=["xT%d" % c, "cst"], writes=["bank%d" % (q % 4)])
                    if q % 2 == 0:
                        P.act(lambda e, q=q, bk2=bk2, n=n, sg=sg: e.copy(out=sg[0:n, q * 512:(q + 1) * 512],
                                                                        in_=bk2[0:n, :]),
                              reads=["bank%d" % (q % 4), "AR"], writes=[sk])
                    else:
                        P.dve(lambda e, q=q, bk2=bk2, n=n, sg=sg: e.tensor_copy(out=sg[0:n, q * 512:(q + 1) * 512],
                                                                               in_=bk2[0:n, :]),
                              reads=["bank%d" % (q % 4), "AR"], writes=[sk])
                store(dst[row0:row0 + n, :], sg[0:n, :], [sk, "AR"])

        def proj_T(col0, n, s_, ks, ncols, bk, bkey, c_off=0):
            for kc in range(KC):
                P.pe(lambda e, kc=kc: e.matmul(bk[0:n, 0:ncols], lhsT=hT[:, kc, col0:col0 + n],
                                               rhs=s_[:, kc, c_off:c_off + ncols], start=(kc == 0), stop=(kc == KC - 1)),
                     reads=[ks, "hT%d" % kc], writes=[bkey])

        def groupnorm_T(src_key, src, n, G, gs, gain, dst_f32, dst_bf, tmp, ss, keys_w, extra_reads=()):
            W_ = G * gs
            P.dve(lambda e: e.tensor_tensor(out=tmp[0:n, 0:W_], in0=src, in1=src, op=ALU.mult),
                  reads=[src_key, "AR"], writes=["gn_tmp"])
            P.dve(lambda e: e.tensor_reduce(out=ss[0:n, 0:G], in_=tmp[0:n, 0:W_].rearrange("p (a b) -> p a b", a=G),
                                            axis=AX.X, op=ALU.add), reads=["gn_tmp", "AR"], writes=["gn_ss"])
            P.act(lambda e: e.activation(out=ss[0:n, G:2 * G], in_=ss[0:n, 0:G], func=AF.Ln, bias=epsc[0:n, 0:1],
                                         scale=1.0 / gs), reads=["gn_ss", "epsc", "AR"], writes=["gn_ss2"])
            P.act(lambda e: e.activation(out=ss[0:n, G:2 * G], in_=ss[0:n, G:2 * G], func=AF.Exp, scale=-0.5),
                  reads=["gn_ss2", "AR"], writes=["gn_ss2"])
            P.dve(lambda e: e.tensor_tensor(out=tmp[0:n, 0:W_].rearrange("p (a b) -> p a b", a=G),
                                            in0=src.rearrange("p (a b) -> p a b", a=G),
                                            in1=ss[0:n, G:2 * G].unsqueeze(2).broadcast_to([n, G, gs]), op=ALU.mult),
                  reads=[src_key, "gn_ss2", "AR"], writes=["gn_tmp"])
            outs = []
            if dst_f32 is not None:
                outs.append(dst_f32)
            if dst_bf is not None:
                outs.append(dst_bf)
            for di, dst in enumerate(outs):
                P.dve(lambda e, dst=dst: e.tensor_tensor(out=dst.rearrange("p (a b) -> p a b", a=G),
                                                        in0=tmp[0:n, 0:W_].rearrange("p (a b) -> p a b", a=G),
                                                        in1=gain[0:n, :].unsqueeze(1).broadcast_to([n, G, gs]),
                                                        op=ALU.mult),
                      reads=["gn_tmp", "gbc", "AR"] + list(extra_reads), writes=[keys_w[di]])

        def transpose_T2F(src_bf, src_key, n, nblk, dst_fn, dst_keys, bank_i):
            bk = banks[bank_i].bitcast(BF16)
            for j in range(nblk):
                P.pe(lambda e, j=j: e.transpose(out=bk[:, j * 128:j * 128 + n], in_=src_bf[0:n, j * 128:(j + 1) * 128],
                                                identity=identb[0:n, 0:n]),
                     reads=[src_key, "identb", "AR"], writes=["bank%d" % bank_i])
            for j in range(nblk):
                if True:
                    P.act(lambda e, j=j: e.copy(out=dst_fn(j), in_=bk[:, j * 128:j * 128 + n]),
                          reads=["bank%d" % bank_i, "AR"], writes=[dst_keys[j]])
                else:
                    P.dve(lambda e, j=j: e.tensor_copy(out=dst_fn(j), in_=bk[:, j * 128:j * 128 + n]),
                          reads=["bank%d" % bank_i, "AR"], writes=[dst_keys[j]])

        def mixer(TW, subs, kind, tile_i):
            sprinkle(10 ** 6)
            norm_F(TW, 1)
            arena_phase()
            NS = len(subs)
            catB = AR.bf([128, 8, TW])
            tproc = AR.f32([128, 512])
            gtmp = AR.f32([128, 512])
            gss_ = AR.f32([128, 16])
            oute = [AR.f32([128, 512]) for _ in range(2)]
            tb = AR.bf([128, 512])
            dqT = AR.bf([128, 4, TW])
            Ebuf = [AR.bf([128, 4, 128]) for _ in range(2)]
            stmp = AR.f32([128, 128])
            dacc = AR.f32([128, 512])
            t1 = AR.f32([128, 128])
            rc = AR.f32([128, 4])
            Enear = [AR.bf([128, 2, 128]) for _ in range(2)]
            stmp2 = [AR.f32([128, 128]) for _ in range(4)]
            en = [0]

            def out_rows(sub):
                return sub["orow"]

            s_, ks = load_w(w_in[:, C_DK:C_DK + 512], 16, 512)
            for si, sub in enumerate(subs):
                proj_T(sub["col0"], sub["n"], s_, ks, 512, banks[si % 4], "bank%d" % (si % 4))
            for si, sub in enumerate(subs):
                n, col0 = sub["n"], sub["col0"]
                bk, bkey = banks[si % 4], "bank%d" % (si % 4)
                P.act(lambda e, bk=bk, n=n: e.copy(out=tproc[0:n, :], in_=bk[0:n, :]), reads=[bkey, "AR"],
                      writes=["tproc"])
                CP("m_dk_a")
                oe = oute[si % 2]
                oek = "oute%d" % (si % 2)
                groupnorm_T("tproc", tproc[0:n, :], n, 8, 64, gbc[:, 64:128], oe[0:n, :], tb[0:n, :], gtmp, gss_,
                            [oek, "tb"])
                CP("m_dk_b")
                store(sub["dk_dst"], oe[0:n, :], [oek, "AR"])
                CP("m_dk_c")
                if kind == "prompt":
                    p0 = sub["pos0"]
                    transpose_T2F(tb, "tb", n, 4, lambda j, p0=p0, n=n: dkT[:, j, p0:p0 + n],
                                  ["dkT"] * 4, 4 + si % 2)
                    CP("m_dk_d")
                else:
                    sq_ = sub["seq"]
                    transpose_T2F(tb, "tb", n, 4, lambda j, sq_=sq_, n=n: dkTn[:, j, sq_ * 32:sq_ * 32 + n],
                                  ["dkTn"] * 4, 4 + si % 2)
            CP("m_dk")
            s_, ks = load_w(w_in[:, C_DV:C_DV + 512], 16, 512)
            for si, sub in enumerate(subs):
                n, col0 = sub["n"], sub["col0"]
                bk, bkey = banks[si % 4], "bank%d" % (si % 4)
                proj_T(col0, n, s_, ks, 512, bk, bkey)
                oe = oute[si % 2]
                oek = "oute%d" % (si % 2)
                P.act(lambda e, bk=bk, n=n, oe=oe: e.copy(out=oe[0:n, :], in_=bk[0:n, :]), reads=[bkey, "AR"],
                      writes=[oek])
                store(sub["dv_dst"], oe[0:n, :], [oek, "AR"])
                CP("m_dv_a")
                if kind == "prompt":
                    blk = sub["pos0"] // 128
                    P.dve(lambda e, oe=oe, n=n, blk=blk: e.tensor_copy(
                        out=vaug[0:n, blk, :, 0:128], in_=oe[0:n, :].rearrange("p (a b) -> p a b", a=4)),
                        reads=[oek, "vaug_init", "AR"], writes=["vaug"])
                    CP("m_dv_b")
                else:
                    sq_ = sub["seq"]
                    P.dve(lambda e, oe=oe, n=n, sq_=sq_: e.tensor_copy(
                        out=vnew[0:n, sq_, :, 0:128], in_=oe[0:n, :].rearrange("p (a b) -> p a b", a=4)),
                        reads=[oek, "vnew_init", "AR"], writes=["vnew"])

            def qproj(c_w, gain, G, gs):
                s2, ks2 = load_w(w_in[:, c_w:c_w + 512], 16, 512)
                for si, sub in enumerate(subs):
                    proj_T(sub["col0"], sub["n"], s2, ks2, 512, banks[si % 4], "bank%d" % (si % 4))
                for si, sub in enumerate(subs):
                    n, col0 = sub["n"], sub["col0"]
                    bk, bkey = banks[si % 4], "bank%d" % (si % 4)
                    P.act(lambda e, bk=bk, n=n: e.copy(out=tproc[0:n, :], in_=bk[0:n, :]), reads=[bkey, "AR"],
                          writes=["tproc"])
                    groupnorm_T("tproc", tproc[0:n, :], n, G, gs, gain, None, tb[0:n, :], gtmp, gss_, ["tb"])
                    transpose_T2F(tb, "tb", n, 4, lambda j, col0=col0, n=n: dqT[:, j, col0:col0 + n],
                                  ["dqT"] * 4, 4 + si % 2)

            def finish_cat(n, col0, gain, normed, cbase):
                if normed:
                    groupnorm_T("dacc", dacc[0:n, :], n, 4, 128, gain, None, tb[0:n, :], gtmp, gss_, ["tb"])
                    CP("fc_a")
                else:
                    P.act(lambda e: e.copy(out=tb[0:n, :], in_=dacc[0:n, :]), reads=["dacc", "AR"], writes=["tb"])
                transpose_T2F(tb, "tb", n, 4, lambda j: catB[:, cbase + j, col0:col0 + n],
                              ["catB%d" % (cbase + j) for j in range(4)], 5)

            CP("m_dv")
            qproj(C_DQ, gbc[:, 0:64], 8, 64)
            CP("m_dq")
            ei = [0]

            class Pipe:
                def __init__(self):
                    self.pending = None

                def push(self, s1, s2):
                    s1()
                    if self.pending is not None:
                        self.pending()
                    self.pending = s2

                def flush(self):
                    if self.pending is not None:
                        self.pending()
                    self.pending = None

            pipe = Pipe()

            def diff_subtile(si, sub):
                n, col0 = sub["n"], sub["col0"]
                blocks = []
                if kind == "prompt":
                    qb = sub["pos0"] // 128
                    for b in range(qb + 1):
                        ty = 0 if b == qb else (1 if b == qb - 1 else 2)
                        blocks.append((("p", b), 128, ty))
                else:
                    sq_ = sub["seq"]
                    for b in range(8):
                        blocks.append((("c", sq_, b), 128, 1 if b == 7 else 2))
                    blocks.append((("n", sq_), n, 0))

                def kT_ap(bd, h, c):
                    if bd[0] == "p":
                        return dkT[c * 64:(c + 1) * 64, h, bd[1] * 128:(bd[1] + 1) * 128]
                    if bd[0] == "c":
                        o = bd[1] * 1024 + bd[2] * 128
                        return dkT[c * 64:(c + 1) * 64, h, o:o + 128]
                    return dkTn[c * 64:(c + 1) * 64, h, bd[1] * 32:bd[1] * 32 + n]

                def v_ap(bd, h):
                    if bd[0] == "p":
                        return vaug[:, bd[1], h, 0:129]
                    if bd[0] == "c":
                        return vaug[:, bd[1] * 8 + bd[2], h, 0:129]
                    return vnew[0:n, bd[1], h, 0:129]

                far_blocks = [b_ for b_ in blocks if b_[2] == 2]
                near_blocks = [b_ for b_ in blocks if b_[2] != 2]
                nblk_all = len(blocks)

                def finalize_head(h, ob, obk):
                    P.dve(lambda e: e.reciprocal(out=rc[0:n, 0:2],
                                                 in_=ob[0:n, :].rearrange("p (a b) -> p a b", a=2)[:, :, 128]),
                          reads=[obk, "AR"], writes=["rc"])
                    P.dve(lambda e: e.tensor_tensor(out=rc[0:n, 2:3], in0=rc[0:n, 1:2], in1=neglam[0:n, 0:1], op=ALU.mult),
                          reads=["rc", "neglam", "AR"], writes=["rc2"])
                    P.act(lambda e: e.activation(out=t1[0:n, :], in_=ob[0:n, 0:128], func=AF.Copy,
                                                 scale=rc[0:n, 0:1]), reads=[obk, "rc", "AR"], writes=["t1"])
                    P.dve(lambda e: e.scalar_tensor_tensor(
                        out=dacc[0:n, h * 128:(h + 1) * 128], in0=ob[0:n, 256:384], scalar=rc[0:n, 2:3], in1=t1[0:n, :],
                        op0=ALU.mult, op1=ALU.add), reads=[obk, "rc2", "t1", "AR"], writes=["dacc"])

                def make_item(h, c, grp, cnt0, is_far, last_of_head, last_of_sub):
                    ob = banks[2 + h % 2]
                    obk = "bank%d" % (2 + h % 2)
                    if is_far:
                        sbk = banks[ei[0] % 2]
                        sbkey = "bank%d" % (ei[0] % 2)
                        E = Ebuf[ei[0] % 2]
                        Ek = "E%d" % (ei[0] % 2)
                        ei[0] += 1
                    else:
                        sbk = banks[6 + en[0] % 2]
                        sbkey = "bank%d" % (6 + en[0] % 2)
                        E = Enear[en[0] % 2]
                        Ek = "En%d" % (en[0] % 2)
                        eni = en[0] % 2
                        en[0] += 1
                    ng = len(grp)

                    def s1():
                        for j, (bd, kn, ty) in enumerate(grp):
                            P.pe(lambda e, bd=bd, kn=kn, j=j: e.matmul(
                                sbk[0:kn, j * 128:j * 128 + n], lhsT=kT_ap(bd, h, c),
                                rhs=dqT[c * 64:(c + 1) * 64, h, col0:col0 + n], start=True, stop=True),
                                reads=["dkT", "dkTn", "dqT", "AR"], writes=[sbkey])
                        if is_far:
                            P.act(lambda e: e.activation(
                                out=E[:, 0:ng, 0:n], in_=sbk[:].rearrange("p (a b) -> p a b", a=4)[:, 0:ng, 0:n],
                                func=AF.Exp, scale=0.125), reads=[sbkey, "AR"], writes=[Ek])
                        else:
                            for j, (bd, kn, ty) in enumerate(grp):
                                stj = stmp2[(eni * 2 + j) % 4]
                                stk = "stmp%d" % ((eni * 2 + j) % 4)
                                P.dve(lambda e, j=j, kn=kn, ty=ty, stj=stj: e.scalar_tensor_tensor(
                                    out=stj[0:kn, 0:n], in0=sbk[0:kn, j * 128:j * 128 + n], scalar=0.125,
                                    in1=BIAS[0:kn, h, ty, 0:n], op0=ALU.mult, op1=ALU.add),
                                    reads=[sbkey, "BIAS", "AR"], writes=[stk])
                                P.act(lambda e, j=j, kn=kn, stj=stj: e.activation(out=E[0:kn, j, 0:n], in_=stj[0:kn, 0:n],
                                                                                  func=AF.Exp),
                                      reads=[stk, "AR"], writes=[Ek])

                    def s2():
                        for j, (bd, kn, ty) in enumerate(grp):
                            cnt = cnt0 + j
                            P.pe(lambda e, bd=bd, kn=kn, j=j, cnt=cnt: e.matmul(
                                ob[0:n, c * 256:c * 256 + 129], lhsT=E[0:kn, j, 0:n], rhs=v_ap(bd, h),
                                start=(cnt == 0), stop=(cnt == nblk_all - 1)),
                                reads=[Ek, "vaug", "vnew", "AR"], writes=[obk])
                        if last_of_head:
                            finalize_head(h, ob, obk)
                        if last_of_sub:
                            finish_cat(n, col0, gbc[:, 384:512], True, 0)
                            CP("dsi%d" % si)
                    return s1, s2

                for h in range(4):
                    for c in range(2):
                        cnt = 0
                        for g0 in range(0, len(far_blocks), 4):
                            grp = far_blocks[g0:g0 + 4]
                            pipe.push(*make_item(h, c, grp, cnt, True, False, False))
                            cnt += len(grp)
                        pipe.push(*make_item(h, c, near_blocks, cnt, False, c == 1, (c == 1 and h == 3)))

            for si, sub in enumerate(subs):
                diff_subtile(si, sub)
                pipe.flush()

            CP("m_diff")
            qproj(C_MQ, gbc[:, 128:256], 4, 128)

            def mem_item(si, sub, h):
                n, col0 = sub["n"], sub["col0"]
                mi = sub["mem"]
                ob = banks[2 + h % 2]
                obk = "bank%d" % (2 + h % 2)
                sbk = banks[ei[0] % 2]
                sbkey = "bank%d" % (ei[0] % 2)
                E = Ebuf[ei[0] % 2]
                Ek = "E%d" % (ei[0] % 2)
                ei[0] += 1

                def s1():
                    for mb in range(2):
                        P.pe(lambda e, mb=mb: e.matmul(
                            sbk[:, mb * 128:mb * 128 + n], lhsT=mkT[mi][:, h, mb * 128:(mb + 1) * 128],
                            rhs=dqT[:, h, col0:col0 + n], start=True, stop=True),
                            reads=["mkT%d" % mi, "dqT", "AR"], writes=[sbkey])
                    P.act(lambda e: e.activation(
                        out=E[:, 0:2, 0:n], in_=sbk[:].rearrange("p (a b) -> p a b", a=4)[:, 0:2, 0:n], func=AF.Exp,
                        scale=float(128 ** -0.5)), reads=[sbkey, "AR"], writes=[Ek])

                def s2():
                    for mb in range(2):
                        P.pe(lambda e, mb=mb: e.matmul(
                            ob[0:n, 0:129], lhsT=E[:, mb, 0:n], rhs=mvaug[mi][:, mb, h, 0:129], start=(mb == 0),
                            stop=(mb == 1)), reads=[Ek, "mvaug%d" % mi, "AR"], writes=[obk])
                    P.dve(lambda e: e.reciprocal(out=rc[0:n, 0:1], in_=ob[0:n, 128:129]), reads=[obk, "AR"],
                          writes=["rc"])
                    P.act(lambda e: e.activation(out=dacc[0:n, h * 128:(h + 1) * 128], in_=ob[0:n, 0:128],
                                                 func=AF.Copy, scale=rc[0:n, 0:1]),
                          reads=[obk, "rc", "AR"], writes=["dacc"])
                    if h == 3:
                        finish_cat(n, col0, None, False, 4)
                return s1, s2

            for si, sub in enumerate(subs):
                for h in range(4):
                    pipe.push(*mem_item(si, sub, h))
                pipe.flush()

            def apply_wo(cat, ckeys, row0):
                for slab in range(4):
                    s3, ks3 = load_w(wo[row0:row0 + 1024, slab * 512:(slab + 1) * 512], 8, 512)
                    for dc in range(4):
                        for cc in range(8):
                            P.pe(lambda e, dc=dc, cc=cc, s3=s3: e.matmul(
                                banks[4 + dc][:, 0:TW], lhsT=s3[:, cc, dc * 128:(dc + 1) * 128], rhs=cat[:, cc, 0:TW],
                                start=(cc == 0), stop=(cc == 7)), reads=[ks3, ckeys[cc], "AR"],
                                writes=["bank%d" % (4 + dc)])
                    for dc in range(4):
                        c = slab * 4 + dc
                        P.dve(lambda e, c=c, dc=dc: e.tensor_tensor(out=xT[:, c, 0:TW], in0=banks[4 + dc][:, 0:TW],
                                                                   in1=xT[:, c, 0:TW], op=ALU.add),
                              reads=["bank%d" % (4 + dc), "xT%d" % c], writes=["xT%d" % c])

            CP("m_mem")
            apply_wo(catB, ["catB%d" % j for j in range(8)], 1024)
            CP("m_woB")

            arena_phase()
            catA = AR.bf([128, 8, TW])
            gqT = AR.bf([128, 4, TW])
            gkT = AR.bf([128, 4, TW])
            gkTok = AR.bf([128, NS, 512])
            gvTok = AR.bf([128, NS, 1024])
            glrT = AR.bf([128, 512])[0:16, :]
            lbuf = AR.f32([128, 512])
            eb = AR.f32([128, 4, 128])
            enb = AR.f32([128, 4, 128])
            qe = AR.bf([128, 4, 128])
            ke = AR.bf([128, 4, 128])
            k2 = AR.bf([128, 512])
            AmT = AR.bf([128, 4, 128])
            rs2 = AR.f32([128, 128])

            for (c_w, dstT, dk_) in ((C_GQ, gqT, "gqT"), (C_GK, gkT, "gkT")):
                s4, ks4 = load_w(w_in[:, c_w:c_w + 512], 16, 512)
                for j in range(4):
                    bk, bkey = banks[j % 4], "bank%d" % (j % 4)
                    for kc in range(KC):
                        P.pe(lambda e, kc=kc, j=j, bk=bk, s4=s4: e.matmul(
                            bk[:, 0:TW], lhsT=s4[:, kc, j * 128:(j + 1) * 128], rhs=hT[:, kc, 0:TW], start=(kc == 0),
                            stop=(kc == KC - 1)), reads=[ks4, "hT%d" % kc], writes=[bkey])
                    P.act(lambda e, j=j, bk=bk, dstT=dstT: e.copy(out=dstT[:, j, 0:TW], in_=bk[:, 0:TW]),
                          reads=[bkey, "AR"], writes=[dk_])
                if c_w == C_GK:
                    for si, sub in enumerate(subs):
                        n, col0 = sub["n"], sub["col0"]
                        bk, bkey = banks[4 + si % 4], "bank%d" % (4 + si % 4)
                        proj_T(col0, n, s4, ks4, 512, bk, bkey)
                        P.act(lambda e, bk=bk, n=n, si=si: e.copy(out=gkTok[0:n, si, :], in_=bk[0:n, :]),
                              reads=[bkey, "AR"], writes=["gkTok"])
            for half in range(2):
                s5, ks5 = load_w(w_in[:, C_GV + half * 512:C_GV + (half + 1) * 512], 16, 512)
                for si, sub in enumerate(subs):
                    n, col0 = sub["n"], sub["col0"]
                    bk, bkey = banks[si % 4], "bank%d" % (si % 4)
                    proj_T(col0, n, s5, ks5, 512, bk, bkey)
                    P.act(lambda e, bk=bk, n=n, si=si, half=half: e.copy(
                        out=gvTok[0:n, si, half * 512:(half + 1) * 512], in_=bk[0:n, :]), reads=[bkey, "AR"],
                        writes=["gvTok"])
            s6, ks6 = load_w(w_in[:, C_GLR:C_GLR + 16], 16, 16)
            for kc in range(KC):
                P.pe(lambda e, kc=kc: e.matmul(banks[4][0:16, 0:TW], lhsT=s6[:, kc, 0:16], rhs=hT[:, kc, 0:TW],
                                               start=(kc == 0), stop=(kc == KC - 1)),
                     reads=[ks6, "hT%d" % kc], writes=["bank4"])
            P.act(lambda e: e.copy(out=glrT[:, 0:TW], in_=banks[4][0:16, 0:TW]), reads=["bank4", "AR"], writes=["glrT"])

            CP("m_gproj")
            for si, sub in enumerate(subs):
                n, col0 = sub["n"], sub["col0"]
                sidx = sub["state"]
                S, Sb = Sst[sidx], Sbf[sidx]
                Sk, Sbk = "Sst%d" % sidx, "Sbf%d" % sidx
                P.pe(lambda e: e.matmul(banks[5][0:n, :], lhsT=glrT[:, col0:col0 + n], rhs=wg2b[:], start=True, stop=False),
                     reads=["glrT", "wg2b", "AR"], writes=["bank5"])
                P.pe(lambda e: e.matmul(banks[5][0:n, :], lhsT=ones1[0:1, 0:n], rhs=bgb[:], start=False, stop=True),
                     reads=["ones1", "bgb"], writes=["bank5"])
                P.act(lambda e: e.activation(out=lbuf[0:n, :], in_=banks[5][0:n, :], func=AF.Exp, scale=-1.0),
                      reads=["bank5", "AR"], writes=["lbuf"])
                P.act(lambda e: e.activation(out=lbuf[0:n, :], in_=lbuf[0:n, :], func=AF.Ln, bias=1.0),
                      reads=["lbuf", "AR"], writes=["lbuf"])
                for h in range(4):
                    P.pe(lambda e, h=h: e.matmul(banks[6][:, h * 128:h * 128 + n], lhsT=lbuf[0:n, h * 128:(h + 1) * 128],
                                                 rhs=TRI[0:n, 0:n], start=True, stop=True),
                         reads=["lbuf", "cst", "AR"], writes=["bank6"])
                P.pe(lambda e: e.matmul(banks[5][0:n, :], lhsT=TRIR[0:n, 0:n], rhs=lbuf[0:n, :], start=True, stop=True),
                     reads=["lbuf", "cst", "AR"], writes=["bank5"])
                b6 = banks[6][:].rearrange("p (a b) -> p a b", a=4)
                P.act(lambda e: e.activation(out=eb[:, :, 0:n], in_=b6[:, :, 0:n], func=AF.Exp), reads=["bank6", "AR"],
                      writes=["eb"])
                P.act(lambda e: e.activation(out=enb[:, :, 0:n], in_=b6[:, :, 0:n], func=AF.Exp, scale=-1.0),
                      reads=["bank6", "AR"], writes=["enb"])
                P.dve(lambda e: e.scalar_tensor_tensor(out=qe[:, :, 0:n], in0=gqT[:, :, col0:col0 + n],
                                                       scalar=float(128 ** -0.5), in1=eb[:, :, 0:n], op0=ALU.mult,
                                                       op1=ALU.mult), reads=["gqT", "eb", "AR"], writes=["qe"])
                P.dve(lambda e: e.tensor_tensor(out=ke[:, :, 0:n], in0=gkT[:, :, col0:col0 + n], in1=enb[:, :, 0:n],
                                                op=ALU.mult), reads=["gkT", "enb", "AR"], writes=["ke"])
                P.act(lambda e: e.activation(out=lbuf[0:n, :], in_=banks[5][0:n, :], func=AF.Exp), reads=["bank5", "AR"],
                      writes=["lbuf"])
                P.dve(lambda e: e.tensor_tensor(out=k2[0:n, :], in0=gkTok[0:n, si, :], in1=lbuf[0:n, :], op=ALU.mult),
                      reads=["gkTok", "lbuf", "AR"], writes=["k2"])
                for h in range(4):
                    P.pe(lambda e, h=h: e.matmul(banks[4][0:n, h * 128:h * 128 + n], lhsT=ke[:, h, 0:n], rhs=qe[:, h, 0:n],
                                                 start=True, stop=True), reads=["ke", "qe", "AR"], writes=["bank4"])
                P.dve(lambda e: e.tensor_tensor(out=AmT[0:n, :, 0:n],
                                                in0=banks[4][0:n, :].rearrange("p (a b) -> p a b", a=4)[:, :, 0:n],
                                                in1=CM[0:n, 0:n].unsqueeze(1).broadcast_to([n, 4, n]), op=ALU.mult),
                      reads=["bank4", "cst", "AR"], writes=["AmT"])
                for h in range(4):
                    for ec in range(2):
                        bi_ = (h * 2 + ec) // 4
                        o_ = ((h * 2 + ec) % 4) * 128
                        P.pe(lambda e, h=h, ec=ec, bi_=bi_, o_=o_: e.matmul(
                            banks[bi_][:, o_:o_ + n], lhsT=gvTok[0:n, si, h * 256 + ec * 128:h * 256 + (ec + 1) * 128],
                            rhs=AmT[0:n, h, 0:n], start=True, stop=False), reads=["gvTok", "AmT", "AR"],
                            writes=["bank%d" % bi_])
                        P.pe(lambda e, h=h, ec=ec, bi_=bi_, o_=o_: e.matmul(
                            banks[bi_][:, o_:o_ + n], lhsT=Sb[:, h, ec * 128:(ec + 1) * 128], rhs=qe[:, h, 0:n],
                            start=False, stop=True), reads=[Sbk, "qe", "AR"], writes=["bank%d" % bi_])
                for h in range(4):
                    ub = banks[2 + h % 2]
                    ubk = "bank%d" % (2 + h % 2)
                    P.pe(lambda e, h=h, ub=ub: e.matmul(ub[:, 0:256], lhsT=k2[0:n, h * 128:(h + 1) * 128],
                                                        rhs=gvTok[0:n, si, h * 256:(h + 1) * 256], start=True, stop=True),
                         reads=["k2", "gvTok", "AR"], writes=[ubk])
                    P.dve(lambda e, h=h, ub=ub: e.scalar_tensor_tensor(
                        out=S[:, h, :], in0=S[:, h, :], scalar=eb[:, h, n - 1:n], in1=ub[:, 0:256], op0=ALU.mult,
                        op1=ALU.add), reads=[ubk, "eb", Sk, Sbk, "AR"], writes=[Sk])
                    P.act(lambda e, h=h: e.copy(out=Sb[:, h, :], in_=S[:, h, :]), reads=[Sk], writes=[Sbk])
                for h in range(4):
                    for ec in range(2):
                        bi_ = (h * 2 + ec) // 4
                        o_ = ((h * 2 + ec) % 4) * 128
                        s = sqb[ec]
                        P.act(lambda e, bi_=bi_, o_=o_, s=s: e.activation(out=s[:, 0:n], in_=banks[bi_][:, o_:o_ + n],
                                                                          func=AF.Square),
                              reads=["bank%d" % bi_, "otok"], writes=["sqb%d" % ec])
                        P.pe(lambda e, ec=ec, s=s: e.matmul(banks[7][:, 0:n], lhsT=ones256[:], rhs=s[:, 0:n],
                                                            start=(ec == 0), stop=(ec == 1)),
                             reads=["ones256", "sqb%d" % ec], writes=["bank7"])
                    P.act(lambda e: e.activation(out=rs2[:, 0:n], in_=banks[7][:, 0:n], func=AF.Ln, bias=epsc[:, 0:1]),
                          reads=["bank7", "epsc", "AR"], writes=["rs2"])
                    P.act(lambda e: e.activation(out=rs2[:, 0:n], in_=rs2[:, 0:n], func=AF.Exp, scale=-0.5),
                          reads=["rs2", "AR"], writes=["rs2"])
                    for ec in range(2):
                        bi_ = (h * 2 + ec) // 4
                        o_ = ((h * 2 + ec) % 4) * 128
                        P.dve(lambda e, h=h, ec=ec, bi_=bi_, o_=o_: e.scalar_tensor_tensor(
                            out=catA[:, h * 2 + ec, col0:col0 + n], in0=banks[bi_][:, o_:o_ + n],
                            scalar=gcols[:, 80 + ec:81 + ec], in1=rs2[:, 0:n], op0=ALU.mult, op1=ALU.mult),
                            reads=["bank%d" % bi_, "gcols", "rs2", "AR"], writes=["catA%d" % (h * 2 + ec), "otok"])
            CP("m_gla")
            for half in range(2):
                s7, ks7 = load_w(w_in[:, C_GR + half * 512:C_GR + (half + 1) * 512], 16, 512)
                for j in range(4):
                    cc = half * 4 + j
                    bk, bkey = banks[j % 4], "bank%d" % (j % 4)
                    for kc in range(KC):
                        P.pe(lambda e, kc=kc, j=j, bk=bk, s7=s7: e.matmul(
                            bk[:, 0:TW], lhsT=s7[:, kc, j * 128:(j + 1) * 128], rhs=hT[:, kc, 0:TW], start=(kc == 0),
                            stop=(kc == KC - 1)), reads=[ks7, "hT%d" % kc], writes=[bkey])
                    t = sgt[j % 2]
                    P.act(lambda e, t=t, bk=bk: e.activation(out=t[:, 0:TW], in_=bk[:, 0:TW], func=AF.Silu),
                          reads=[bkey], writes=["sgt0"])
                    P.dve(lambda e, t=t, cc=cc: e.tensor_tensor(out=catA[:, cc, 0:TW], in0=catA[:, cc, 0:TW],
                                                               in1=t[:, 0:TW], op=ALU.mult),
                          reads=["sgt0", "catA%d" % cc, "AR"], writes=["catA%d" % cc])
            apply_wo(catA, ["catA%d" % j for j in range(8)], 0)

        def mem_kv():
            arena_phase()
            load_x_tile(mem, [(0, 0, 128), (128, 128, 128)])
            CP("mem_load")
            norm_F(256, 4)
            CP("mem_norm")
            arena_phase()
            tproc = AR.f32([128, 512])
            gtmp = AR.f32([128, 512])
            gss_ = AR.f32([128, 16])
            oute = [AR.f32([128, 512]) for _ in range(2)]
            tb = AR.bf([128, 512])
            sk_, kk = load_w(wmkv[:, 0:512], 16, 512)
            sv_, kv = load_w(wmkv[:, 512:1024], 16, 512)
            for s in range(2):
                bk, bkey = banks[s], "bank%d" % s
                proj_T(s * 128, 128, sk_, kk, 512, bk, bkey)
                P.act(lambda e, bk=bk: e.copy(out=tproc[:, :], in_=bk[:, :]), reads=[bkey, "AR"], writes=["tproc"])
                CP("mem_proj")
                oe = oute[0]
                groupnorm_T("tproc", tproc[:, :], 128, 4, 128, gbc[:, 256:384], oe[:, :], tb[:, :], gtmp, gss_,
                            ["oute0", "tb"])
                CP("mem_gn")
                store(mkp[s * 128:(s + 1) * 128, :], oe[:, :], ["oute0", "AR"])
                CP("mem_st")
                transpose_T2F(tb, "tb", 128, 4, lambda j, s=s: mkT[0][:, j, s * 128:(s + 1) * 128], ["mkT0"] * 4, 4 + s)
                CP("mem_tr")
                bk2, bkey2 = banks[2 + s], "bank%d" % (2 + s)
                proj_T(s * 128, 128, sv_, kv, 512, bk2, bkey2)
                oe2 = oute[1]
                P.act(lambda e, bk2=bk2, oe2=oe2: e.copy(out=oe2[:, :], in_=bk2[:, :]), reads=[bkey2, "AR"],
                      writes=["oute1"])
                store(mvp[s * 128:(s + 1) * 128, :], oe2[:, :], ["oute1", "AR"])
                CP("mem_v0a")
                P.dve(lambda e, oe2=oe2, s=s: e.tensor_copy(out=mvaug[0][:, s, :, 0:128],
                                                            in_=oe2[:, :].rearrange("p (a b) -> p a b", a=4)),
                      reads=["oute1", "AR"], writes=["mvaug0"])
                CP("mem_v%d" % s)

        def sample_prologue():
            arena_phase()
            AR.limit = AR_COLS - 5152
            o = AR.limit
            Sst[1] = arena[:, o:o + 2048].bitcast(F32).rearrange("p (a b) -> p a b", a=4)
            Sbf[1] = arena[:, o + 2048:o + 3072].rearrange("p (a b) -> p a b", a=4)
            mkT[1] = arena[:, o + 3072:o + 4096].rearrange("p (a b) -> p a b", a=4)
            mvaug[1] = arena[:, o + 4096:o + 5152].rearrange("p (a b c) -> p a b c", a=2, b=4)
            P.pool(lambda e: e.memset(arena[:, o + 4096:o + 5152], 1.0), reads=["AR"], writes=["mvaug1"])
            kst = AR.bf([128, 8, 512])
            for sq_ in range(2):
                P.dma("pool", "wq", lambda e, sq_=sq_: e.dma_start(
                    out=kst, in_=cdk[sq_].rearrange("(b p) c -> p b c", p=128)), reads=["AR"], writes=["kst"])
                for b in range(8):
                    o = sq_ * 1024 + b * 128
                    transpose_T2F(kst[:, b, :], "kst", 128, 4, lambda j, o=o: dkT[:, j, o:o + 128], ["dkT"] * 4,
                                  4 + b % 2)
                for b in range(8):
                    P.dma("pool", "wq", lambda e, sq_=sq_, b=b: e.dma_start(
                        out=vaug[:, sq_ * 8 + b, :, 0:128],
                        in_=cdv[sq_, b * 128:(b + 1) * 128, :].rearrange("p (h d) -> p h d", h=4)),
                        reads=["vaug_init"], writes=["vaug"])
                P.dma("pool", "wq", lambda e, sq_=sq_: e.dma_start(
                    out=kst[:, 0:2, :], in_=cmk[sq_].rearrange("(b p) c -> p b c", p=128)), reads=["AR"], writes=["kst"])
                for b in range(2):
                    transpose_T2F(kst[:, b, :], "kst", 128, 4, lambda j, b=b, sq_=sq_: mkT[sq_][:, j, b * 128:(b + 1) * 128],
                                  ["mkT%d" % sq_] * 4, 4 + b % 2)
                for b in range(2):
                    P.dma("pool", "wq", lambda e, sq_=sq_, b=b: e.dma_start(
                        out=mvaug[sq_][:, b, :, 0:128],
                        in_=cmv[sq_, b * 128:(b + 1) * 128, :].rearrange("p (h d) -> p h d", h=4)),
                        reads=["AR"], writes=["mvaug%d" % sq_])
                load(Sst[sq_][:], sgla[sq_].rearrange("h d e -> d h e"), ["Sst%d" % sq_], reads=["AR"])
                P.act(lambda e, sq_=sq_: e.copy(out=Sbf[sq_][:].rearrange("p a b -> p (a b)"),
                                                in_=Sst[sq_][:].rearrange("p a b -> p (a b)")),
                      reads=["Sst%d" % sq_], writes=["Sbf%d" % sq_])

        def main_schedule():
            for _ in range(pad):
                if padeng == "dve":
                    P.dve(lambda e: e.tensor_copy(out=neglam[:, 3:4], in_=neglam[:, 3:4]))
                else:
                    P.pe(lambda e: e.nop())
            CP("setup")
            if do_mem:
                mem_kv()
            P.pool(lambda e: e.memset(Sst[0][:].rearrange("p a b -> p (a b)"), 0.0), writes=["Sst0"])
            P.pool(lambda e: e.memset(Sbf[0][:].rearrange("p a b -> p (a b)"), 0.0), writes=["Sbf0"])
            for ti in range(n_prompt_tiles):
                wpass["mode"] = "first" if do_sample else "once"
                wpass["nt"] = n_prompt_tiles
                wpass["ti"] = ti
                wpass["k"] = 0
                arena_phase()
                r0 = ti * 512
                load_x_tile(xp, [(r0 + s * 128, s * 128, 128) for s in range(4)])
                CP("t_load")
                ffn(512, w1i, w1o, 0)
                CP("t_ffn1")
                subs = [dict(n=128, col0=s * 128, pos0=r0 + s * 128, seq=0, mem=0, state=0,
                             dk_dst=dkp[r0 + s * 128:r0 + (s + 1) * 128, :], dv_dst=dvp[r0 + s * 128:r0 + (s + 1) * 128, :])
                        for s in range(4)]
                mixer(512, subs, "prompt", ti)
                CP("t_mix")
                ffn(512, w2i, w2o, 2)
                CP("t_ffn2")
                store_y_tile(yp, [(r0 + s * 128, s * 128, 128) for s in range(4)], 512, 3)
                flush_wb()
                assert wpass["k"] == NW_LOADS, wpass["k"]
            store(gsp.rearrange("h d e -> d h e"), Sst[0][:], ["Sst0"])
            if do_sample:
                wpass["mode"] = "later" if n_prompt_tiles > 0 else "once"
                wpass["k"] = 0
                sample_prologue()
                arena_phase()
                load_x_tile(xs, [(0, 0, 64)])
                ffn(64, w1i, w1o, 0)
                subs = [dict(n=32, col0=s * 32, pos0=PAST, seq=s, mem=s, state=s, dk_dst=dks[s * 32:(s + 1) * 32, :],
                             dv_dst=dvs[s * 32:(s + 1) * 32, :]) for s in range(2)]
                mixer(64, subs, "sample", 0)
                ffn(64, w2i, w2o, 2)
                store_y_tile(ys, [(0, 0, 64)], 64, 3)
                for s in range(2):
                    store(gss[s].rearrange("h d e -> d h e"), Sst[s][:], ["Sst%d" % s])

        try:
            main_schedule()
        except _StopBuild:
            pass
        fin = sb("fin", [128, 8])
        P.pe(lambda e: e.matmul(banks[7][0:1, 0:8], lhsT=ones1[0:1, 0:1], rhs=ones1[0:1, 0:8], start=True, stop=True),
             reads=["ones1"], writes=["bank7"])
        P.act(lambda e: e.copy(out=fin[:, 0:1], in_=epsc[:, 0:1]), reads=["epsc", "bank7"], writes=["fin_act"])
        P.dve(lambda e: e.tensor_copy(out=fin[:, 1:2], in_=epsc[:, 0:1]), reads=["epsc"], writes=["fin_dve"])
        P.pool(lambda e: e.memset(fin[:, 2:3], 0.0), writes=["fin_pool"])
        P.add("sp", lambda e: e.nop(), reads=list(out_keys) + ["fin_act", "fin_dve", "fin_pool"])
        stats = P.emit(nc, st)
    if DO_COMPILE:
        nc.compile()
    return nc, stats


_CACHE = {}


def kernel(x_prompt, x_sample, mem_prompt, cache_diff_k, cache_diff_v, state_gla, cache_mem_k, cache_mem_v,
           rel_bias_table, norm_ffn1, w_ffn1_in, w_ffn1_out, norm_mix, w_in, w_gla_g2, b_gla_g, gla_out_norm,
           diff_q_norm, diff_k_norm, diff_lambda, diff_out_norm, mem_norm, w_mem_kv, mem_q_norm, mem_k_norm,
           w_o, norm_ffn2, w_ffn2_in, w_ffn2_out, norm_final, _cfg=None):
    f = lambda a: np.ascontiguousarray(np.asarray(a, dtype=np.float32))
    cfg = _cfg or dict(n_prompt_tiles=4, do_sample=True, do_mem=True)
    key = tuple(sorted(cfg.items()))
    if key not in _CACHE:
        _CACHE[key] = build_program(**{k: v for k, v in cfg.items() if k != "cores"})
        print("stats", _CACHE[key][1])
    nc, stats = _CACHE[key]
    nvec = np.stack([f(norm_ffn1)[0], f(norm_mix)[0], f(norm_ffn2)[0], f(norm_final)[0], f(mem_norm)[0]])
    shared = dict(
        table=f(rel_bias_table).reshape(128), nvec=f(nvec), glan=f(gla_out_norm)[0],
        w1i=f(w_ffn1_in)[0], w1o=f(w_ffn1_out)[0], w_in=f(w_in)[0], wg2=f(w_gla_g2)[0], bg=f(b_gla_g)[0][None, :],
        dqn=f(diff_q_norm)[0], dkn=f(diff_k_norm)[0], dlam=f(diff_lambda)[0].reshape(256), don=f(diff_out_norm)[0],
        wmkv=f(w_mem_kv)[0], mqn=f(mem_q_norm)[0], mkn=f(mem_k_norm)[0], wo=f(w_o)[0], w2i=f(w_ffn2_in)[0],
        w2o=f(w_ffn2_out)[0], cst=_consts())
    xp_, xs_, mem_ = f(x_prompt), f(x_sample), f(mem_prompt)
    cdk_, cdv_, sg_ = f(cache_diff_k)[0], f(cache_diff_v)[0], f(state_gla)[0]
    cmk_, cmv_ = f(cache_mem_k)[0], f(cache_mem_v)[0]
    in_maps = []
    for c in range(8):
        m = dict(shared)
        m.update(xp=xp_[c], xs=xs_[2 * c:2 * c + 2].reshape(64, D), mem=mem_[c],
                 cdk=cdk_[2 * c:2 * c + 2].reshape(2, PAST, 512), cdv=cdv_[2 * c:2 * c + 2].reshape(2, PAST, 512),
                 sgla=sg_[2 * c:2 * c + 2], cmk=cmk_[2 * c:2 * c + 2].reshape(2, NMEM, 512),
                 cmv=cmv_[2 * c:2 * c + 2].reshape(2, NMEM, 512))
        in_maps.append(m)
    ncores = cfg.get("cores", 8) if _cfg else 8
    if cfg.get("tiny"):
        for m in in_maps:
            for k in ("w1i", "w1o", "w_in", "wmkv", "wo", "w2i", "w2o", "xp"):
                m[k] = np.ascontiguousarray(m[k][:128, :128])
    res = run_bass_kernel_spmd(nc, in_maps[:ncores], core_ids=list(range(ncores)))
    R = list(res.results) + [res.results[0]] * (8 - ncores)
    g = lambda name: np.stack([np.asarray(R[c][name], dtype=np.float32) for c in range(8)])
    y_p = g("yp")
    y_s = g("ys").reshape(16, 32, D)
    dk_p = g("dkp").reshape(1, 8, SEQ, 4, 128)
    dv_p = g("dvp").reshape(1, 8, SEQ, 4, 128)
    gs_p = g("gsp").reshape(1, 8, 4, 128, 256)
    mk_p = g("mkp").reshape(1, 8, NMEM, 4, 128)
    mv_p = g("mvp").reshape(1, 8, NMEM, 4, 128)
    dk_s = g("dks").reshape(1, 16, 32, 4, 128)
    dv_s = g("dvs").reshape(1, 16, 32, 4, 128)
    gs_s = g("gss").reshape(1, 16, 4, 128, 256)
    return (y_p, y_s, dk_p, dv_p, gs_p, mk_p, mv_p, dk_s, dv_s, gs_s)
```
